# Optimizing a Trainium2 kernel written in Bass

```python
import math
import jax
import jax.numpy as jnp
from jax import lax
import numpy as np

D_MODEL = 2048
BATCH = 4
SEQ = 2048
DEPTH = 1
DEC_BATCH = 16
DEC_SEQ = 16
PAST_LEN = 1024

CHUNK = 64
QBLK = 128
N_DIFF_HEADS = 8
DIFF_HEAD_DIM = 128
DIFF_VDIM = 2 * DIFF_HEAD_DIM
DIFF_WIDTH = N_DIFF_HEADS * 2 * DIFF_HEAD_DIM
N_GDN_HEADS = 16
GDN_DK = 128
GDN_DV = 128
GDN_QKWIDTH = N_GDN_HEADS * GDN_DK
GDN_VWIDTH = N_GDN_HEADS * GDN_DV
CONV_W = 4
CONV_CH = 2 * GDN_QKWIDTH + GDN_VWIDTH
N_MEM = 256
N_XHEADS = 4
XHEAD_DIM = 128
XWIDTH = N_XHEADS * XHEAD_DIM
D_FF = ((8 * D_MODEL // 3 + 255) // 256) * 256
ALPHA = (2.0 * DEPTH) ** 0.25
BETA = (8.0 * DEPTH) ** -0.25
LN_EPS = 1e-5
NORM_EPS = 1e-6
IN_SIZES = (DIFF_WIDTH, DIFF_WIDTH, N_DIFF_HEADS * DIFF_VDIM,
            GDN_QKWIDTH, GDN_QKWIDTH, GDN_VWIDTH, GDN_VWIDTH,
            N_GDN_HEADS, N_GDN_HEADS, D_MODEL, D_MODEL)
D_IN = sum(IN_SIZES)

kernel_name = "hybrid_streaming_encoder_step"


def layer_norm(x, g, b):
    xf = x.astype(jnp.float32)
    mu = jnp.mean(xf, -1, keepdims=True)
    var = jnp.mean(jnp.square(xf - mu), -1, keepdims=True)
    return ((xf - mu) * lax.rsqrt(var + LN_EPS) * g + b).astype(x.dtype)


def rms_norm(x, g):
    xf = x.astype(jnp.float32)
    return (xf * lax.rsqrt(jnp.mean(xf * xf, -1, keepdims=True) + NORM_EPS) * g).astype(x.dtype)


def l2norm(x):
    xf = x.astype(jnp.float32)
    return (xf * lax.rsqrt(jnp.sum(xf * xf, -1, keepdims=True) + NORM_EPS)).astype(x.dtype)


def alibi_slopes(n):
    return jnp.exp2(-8.0 * jnp.arange(1, n + 1, dtype=jnp.float32) / n)


def split_in(proj):
    return jnp.split(proj, np.cumsum(IN_SIZES)[:-1].tolist(), axis=-1)


def causal_conv(x, buf, w):
    T = x.shape[1]
    xp = jnp.concatenate([buf, x], axis=1)
    y = sum(w[j] * xp[:, j:j + T] for j in range(CONV_W))
    return y, xp[:, -(CONV_W - 1):]


def diff_attention_block(q, k, v, q_pos, k_pos, lam):
    s = jnp.einsum("bqhmd,bkhmd->bhmqk", q, k).astype(jnp.float32) * (DIFF_HEAD_DIM ** -0.5)
    dist = jnp.abs(q_pos[:, None] - k_pos[None, :]).astype(jnp.float32)
    bias = -alibi_slopes(N_DIFF_HEADS)[:, None, None, None] * dist
    allowed = (k_pos[None, :] // CHUNK) <= (q_pos[:, None] // CHUNK)
    p = jax.nn.softmax(jnp.where(allowed, s + bias, -jnp.inf), axis=-1)
    wts = p[:, :, 0] - lam * p[:, :, 1]
    return jnp.einsum("bhqk,bkhe->bqhe", wts.astype(v.dtype), v)


def diff_attention_prompt(q, k, v, lam):
    B, S = q.shape[0], q.shape[1]
    nb = S // QBLK
    k_pos = jnp.arange(S)
    qb = jnp.moveaxis(q.reshape(B, nb, QBLK, N_DIFF_HEADS, 2, DIFF_HEAD_DIM), 1, 0)

    def one_block(args):
        i, qi = args
        return diff_attention_block(qi, k, v, i * QBLK + jnp.arange(QBLK), k_pos, lam)

    o = lax.map(one_block, (jnp.arange(nb), qb))
    return jnp.moveaxis(o, 0, 1).reshape(B, S, N_DIFF_HEADS, DIFF_VDIM)


def gated_delta_chunked(q, k, v, g, beta, s0, c):
    f32 = jnp.float32
    B, T, H, DK = q.shape
    DV = v.shape[-1]
    n = T // c

    def chunks(a):
        return jnp.swapaxes(a.astype(f32).reshape(B, n, c, *a.shape[2:]), 2, 3)

    qc = chunks(q) * (DK ** -0.5)
    kc = chunks(k)
    vc = chunks(v)
    gc = jnp.cumsum(chunks(g), axis=-1)
    bc = chunks(beta)
    tri = jnp.tril(jnp.ones((c, c), dtype=bool))
    strict = jnp.tril(jnp.ones((c, c), dtype=bool), -1)
    decay = jnp.exp(jnp.where(tri, gc[..., :, None] - gc[..., None, :], -jnp.inf))
    kb = kc * bc[..., None]
    m = jnp.where(strict, jnp.einsum("bnhid,bnhjd->bnhij", kb, kc) * decay, 0.0)
    a = m + jnp.eye(c, dtype=f32)
    rhs = jnp.concatenate([vc * bc[..., None], kb * jnp.exp(gc)[..., None]], axis=-1)
    sol = lax.linalg.triangular_solve(a, rhs, left_side=True, lower=True, unit_diagonal=True)
    u, w = sol[..., :DV], sol[..., DV:]
    qk = jnp.where(tri, jnp.einsum("bnhid,bnhjd->bnhij", qc, kc) * decay, 0.0)

    def step(S, xs):
        q_i, k_i, u_i, w_i, qk_i, g_i = xs
        v_new = u_i - jnp.einsum("bhcd,bhde->bhce", w_i, S)
        o = (jnp.einsum("bhcd,bhde->bhce", q_i * jnp.exp(g_i)[..., None], S)
             + jnp.einsum("bhij,bhje->bhie", qk_i, v_new))
        g_last = g_i[..., -1]
        S = (S * jnp.exp(g_last)[..., None, None]
             + jnp.einsum("bhcd,bhce->bhde", k_i * jnp.exp(g_last[..., None] - g_i)[..., None], v_new))
        return S, o

    xs = tuple(jnp.swapaxes(t, 0, 1) for t in (qc, kc, u, w, qk, gc))
    s_fin, o = lax.scan(step, s0.astype(f32), xs)
    o = jnp.swapaxes(jnp.swapaxes(o, 0, 1), 2, 3).reshape(B, T, H, DV)
    return o.astype(v.dtype), s_fin.astype(s0.dtype)


def encoder_layer(x, mem_k, mem_v, past, lam_init,
                  w_in, conv_w, lam_q1, lam_k1, lam_q2, lam_k2, diff_subln_g,
                  gdn_a_log, gdn_dt_bias, gdn_norm_g, w_pa, w_pb, w_o, ln1_g, ln1_b,
                  w_xq, w_xo, ln2_g, ln2_b, w_ff1, w_ff3, w_ff2, ln3_g, ln3_b):
    B, T, _ = x.shape
    proj = jnp.einsum("btd,de->bte", x, w_in)
    dq, dk, dv, gq, gk, gv, gz, ga, gb, gate_a, gate_b = split_in(proj)

    qd = dq.reshape(B, T, N_DIFF_HEADS, 2, DIFF_HEAD_DIM)
    kd = dk.reshape(B, T, N_DIFF_HEADS, 2, DIFF_HEAD_DIM)
    vd = dv.reshape(B, T, N_DIFF_HEADS, DIFF_VDIM)
    f32 = jnp.float32
    lam = (jnp.exp(jnp.sum(lam_q1.astype(f32) * lam_k1.astype(f32)))
           - jnp.exp(jnp.sum(lam_q2.astype(f32) * lam_k2.astype(f32))) + lam_init)
    new_k = kd.reshape(B, T, N_DIFF_HEADS, 2 * DIFF_HEAD_DIM)
    if past is None:
        o_a = diff_attention_prompt(qd, kd, vd, lam)
        buf0 = jnp.zeros((B, CONV_W - 1, CONV_CH), proj.dtype)
        s0 = jnp.zeros((B, N_GDN_HEADS, GDN_DK, GDN_DV), x.dtype)
        c = CHUNK
    else:
        cache_k, cache_v, s0, buf0 = past
        P = cache_k.shape[1]
        k_all = jnp.concatenate([cache_k, new_k], axis=1).reshape(B, P + T, N_DIFF_HEADS, 2, DIFF_HEAD_DIM)
        v_all = jnp.concatenate([cache_v, vd], axis=1)
        o_a = diff_attention_block(qd, k_all, v_all, P + jnp.arange(T), jnp.arange(P + T), lam)
        c = T
    o_a = (rms_norm(o_a, diff_subln_g) * (1.0 - lam_init)).reshape(B, T, DIFF_WIDTH)

    qkv, new_buf = causal_conv(jnp.concatenate([gq, gk, gv], axis=-1), buf0, conv_w)
    qkv = jax.nn.silu(qkv)
    q_g, k_g, v_g = jnp.split(qkv, [GDN_QKWIDTH, 2 * GDN_QKWIDTH], axis=-1)
    q_g = l2norm(q_g.reshape(B, T, N_GDN_HEADS, GDN_DK))
    k_g = l2norm(k_g.reshape(B, T, N_GDN_HEADS, GDN_DK))
    v_g = v_g.reshape(B, T, N_GDN_HEADS, GDN_DV)
    g = -jnp.exp(gdn_a_log.astype(f32)) * jax.nn.softplus(ga.astype(f32) + gdn_dt_bias.astype(f32))
    beta = jax.nn.sigmoid(gb.astype(f32))
    o_b, s_new = gated_delta_chunked(q_g, k_g, v_g, g, beta, s0, c)
    o_b = (rms_norm(o_b, gdn_norm_g) * jax.nn.silu(gz.reshape(B, T, N_GDN_HEADS, GDN_DV))).reshape(B, T, GDN_VWIDTH)

    mix = (jax.nn.sigmoid(gate_a) * jnp.einsum("bte,ed->btd", o_a, w_pa)
           + jax.nn.sigmoid(gate_b) * jnp.einsum("bte,ed->btd", o_b, w_pb))
    h1 = layer_norm(ALPHA * x + jnp.einsum("btd,de->bte", mix, w_o), ln1_g, ln1_b)

    qx = jnp.einsum("btd,de->bte", h1, w_xq).reshape(B, T, N_XHEADS, XHEAD_DIM)
    sx = jnp.einsum("bthd,bmhd->bhtm", qx, mem_k).astype(f32) * (XHEAD_DIM ** -0.5)
    px = jax.nn.softmax(sx, axis=-1)
    ox = jnp.einsum("bhtm,bmhd->bthd", px.astype(mem_v.dtype), mem_v).reshape(B, T, XWIDTH)
    h2 = layer_norm(ALPHA * h1 + jnp.einsum("bte,ed->btd", ox, w_xo), ln2_g, ln2_b)

    f = jax.nn.silu(jnp.einsum("btd,df->btf", h2, w_ff1)) * jnp.einsum("btd,df->btf", h2, w_ff3)
    y = layer_norm(ALPHA * h2 + jnp.einsum("btf,fd->btd", f, w_ff2), ln3_g, ln3_b)
    return y, new_k, vd, s_new, new_buf


def setup_inputs(seed: int = 0) -> dict:
    key = jax.random.key(seed)
    ks = iter(jax.random.split(key, 64))
    L, D = DEPTH, D_MODEL

    def nrm(shape, s):
        return jax.random.normal(next(ks), shape, jnp.float32) * s

    def gain(n):
        return 1.0 + nrm((L, n), 0.02)

    x_prompt = nrm((BATCH, SEQ, D), 1.0)
    x_sample = nrm((DEC_BATCH, DEC_SEQ, D), 1.0)
    mem_prompt = nrm((BATCH, N_MEM, D), 1.0)
    cache_diff_k = nrm((L, DEC_BATCH, PAST_LEN, N_DIFF_HEADS, 2 * DIFF_HEAD_DIM), 1.0)
    cache_diff_v = nrm((L, DEC_BATCH, PAST_LEN, N_DIFF_HEADS, DIFF_VDIM), 0.5)
    state_gdn = nrm((L, DEC_BATCH, N_GDN_HEADS, GDN_DK, GDN_DV), 0.1)
    state_gdn_conv = nrm((L, DEC_BATCH, CONV_W - 1, CONV_CH), 1.0)
    cache_mem_k = nrm((L, DEC_BATCH, N_MEM, N_XHEADS, XHEAD_DIM), 1.0)
    cache_mem_v = nrm((L, DEC_BATCH, N_MEM, N_XHEADS, XHEAD_DIM), 0.5)

    s_in = D ** -0.5
    w_in = jnp.concatenate([
        nrm((L, D, DIFF_WIDTH), s_in),
        nrm((L, D, DIFF_WIDTH), s_in),
        nrm((L, D, N_DIFF_HEADS * DIFF_VDIM), s_in * BETA),
        nrm((L, D, GDN_QKWIDTH), s_in),
        nrm((L, D, GDN_QKWIDTH), s_in),
        nrm((L, D, GDN_VWIDTH), s_in * BETA),
        nrm((L, D, GDN_VWIDTH), s_in),
        nrm((L, D, N_GDN_HEADS), s_in),
        nrm((L, D, N_GDN_HEADS), s_in),
        nrm((L, D, D), s_in),
        nrm((L, D, D), s_in),
    ], axis=-1)
    conv_w = nrm((L, CONV_W, CONV_CH), CONV_W ** -0.5)
    lam_q1 = nrm((L, DIFF_HEAD_DIM), 0.1)
    lam_k1 = nrm((L, DIFF_HEAD_DIM), 0.1)
    lam_q2 = nrm((L, DIFF_HEAD_DIM), 0.1)
    lam_k2 = nrm((L, DIFF_HEAD_DIM), 0.1)
    diff_subln_g = gain(DIFF_VDIM)
    gdn_a_log = jnp.log(jax.random.uniform(next(ks), (L, N_GDN_HEADS), jnp.float32, 1.0, 16.0))
    dt = jnp.exp(jax.random.uniform(next(ks), (L, N_GDN_HEADS), jnp.float32,
                                    math.log(1e-3), math.log(1e-1)))
    gdn_dt_bias = dt + jnp.log(-jnp.expm1(-dt))
    gdn_norm_g = gain(GDN_DV)
    w_pa = nrm((L, DIFF_WIDTH, D), DIFF_WIDTH ** -0.5 * BETA)
    w_pb = nrm((L, GDN_VWIDTH, D), GDN_VWIDTH ** -0.5 * BETA)
    w_o = nrm((L, D, D), D ** -0.5 * BETA)
    ln1_g = gain(D)
    ln1_b = nrm((L, D), 0.02)
    w_xq = nrm((L, D, XWIDTH), s_in)
    w_xk = nrm((L, D, XWIDTH), s_in)
    w_xv = nrm((L, D, XWIDTH), s_in * BETA)
    w_xo = nrm((L, XWIDTH, D), XWIDTH ** -0.5 * BETA)
    ln2_g = gain(D)
    ln2_b = nrm((L, D), 0.02)
    w_ff1 = nrm((L, D, D_FF), s_in * BETA)
    w_ff3 = nrm((L, D, D_FF), s_in * BETA)
    w_ff2 = nrm((L, D_FF, D), D_FF ** -0.5 * BETA)
    ln3_g = gain(D)
    ln3_b = nrm((L, D), 0.02)
    return {
        "x_prompt": x_prompt, "x_sample": x_sample, "mem_prompt": mem_prompt,
        "cache_diff_k": cache_diff_k, "cache_diff_v": cache_diff_v,
        "state_gdn": state_gdn, "state_gdn_conv": state_gdn_conv,
        "cache_mem_k": cache_mem_k, "cache_mem_v": cache_mem_v,
        "w_in": w_in, "conv_w": conv_w,
        "lam_q1": lam_q1, "lam_k1": lam_k1, "lam_q2": lam_q2, "lam_k2": lam_k2,
        "diff_subln_g": diff_subln_g, "gdn_a_log": gdn_a_log, "gdn_dt_bias": gdn_dt_bias,
        "gdn_norm_g": gdn_norm_g, "w_pa": w_pa, "w_pb": w_pb, "w_o": w_o,
        "ln1_g": ln1_g, "ln1_b": ln1_b,
        "w_xq": w_xq, "w_xk": w_xk, "w_xv": w_xv, "w_xo": w_xo,
        "ln2_g": ln2_g, "ln2_b": ln2_b,
        "w_ff1": w_ff1, "w_ff3": w_ff3, "w_ff2": w_ff2,
        "ln3_g": ln3_g, "ln3_b": ln3_b,
    }


def reference(x_prompt, x_sample, mem_prompt, cache_diff_k, cache_diff_v, state_gdn, state_gdn_conv,
              cache_mem_k, cache_mem_v, w_in, conv_w, lam_q1, lam_k1, lam_q2, lam_k2, diff_subln_g,
              gdn_a_log, gdn_dt_bias, gdn_norm_g, w_pa, w_pb, w_o, ln1_g, ln1_b,
              w_xq, w_xk, w_xv, w_xo, ln2_g, ln2_b, w_ff1, w_ff3, w_ff2, ln3_g, ln3_b):
    yp, ys = x_prompt, x_sample
    Bp = x_prompt.shape[0]
    pk_l, pv_l, ps_l, pc_l, mk_l, mv_l = [], [], [], [], [], []
    sk_l, sv_l, ss_l, sc_l = [], [], [], []
    for l in range(DEPTH):
        lam_init = 0.8 - 0.6 * math.exp(-0.3 * l)
        lw = (w_in[l], conv_w[l], lam_q1[l], lam_k1[l], lam_q2[l], lam_k2[l], diff_subln_g[l],
              gdn_a_log[l], gdn_dt_bias[l], gdn_norm_g[l], w_pa[l], w_pb[l], w_o[l], ln1_g[l], ln1_b[l],
              w_xq[l], w_xo[l], ln2_g[l], ln2_b[l], w_ff1[l], w_ff3[l], w_ff2[l], ln3_g[l], ln3_b[l])
        mem_k = jnp.einsum("bmd,de->bme", mem_prompt, w_xk[l]).reshape(Bp, N_MEM, N_XHEADS, XHEAD_DIM)
        mem_v = jnp.einsum("bmd,de->bme", mem_prompt, w_xv[l]).reshape(Bp, N_MEM, N_XHEADS, XHEAD_DIM)
        yp, pk, pv, ps, pc = encoder_layer(yp, mem_k, mem_v, None, lam_init, *lw)
        past = (cache_diff_k[l], cache_diff_v[l], state_gdn[l], state_gdn_conv[l])
        ys, sk, sv, ss, sc = encoder_layer(ys, cache_mem_k[l], cache_mem_v[l], past, lam_init, *lw)
        pk_l.append(pk); pv_l.append(pv); ps_l.append(ps); pc_l.append(pc)
        mk_l.append(mem_k); mv_l.append(mem_v)
        sk_l.append(sk); sv_l.append(sv); ss_l.append(ss); sc_l.append(sc)
    return (yp, ys,
            jnp.stack(pk_l), jnp.stack(pv_l), jnp.stack(ps_l), jnp.stack(pc_l),
            jnp.stack(mk_l), jnp.stack(mv_l),
            jnp.stack(sk_l), jnp.stack(sv_l), jnp.stack(ss_l), jnp.stack(sc_l))
```

```python
import contextlib
import numpy as np
import concourse.bass as bass
import concourse.mybir as mybir
from concourse.bass_utils import run_bass_kernel_spmd

F32 = mybir.dt.float32
BF16 = mybir.dt.bfloat16
F32R = mybir.dt.float32r
AF = mybir.ActivationFunctionType
ALU = mybir.AluOpType

D = 2048
NPR = 2048
NS = 2
TS = 16
T = NPR + NS * TS
PAST = 1024
BLOCKS = [(i * 128, 128) for i in range(16)] + [(NPR + TS * s, TS) for s in range(NS)]
NB = len(BLOCKS)
SPANS = [(0, 512), (512, 512), (1024, 512), (1536, 512), (2048, NS * TS)]
DQ, DK, DV, GQ, GK, GV, GZ, GA, GB, GTA, GTB = 0, 2048, 4096, 6144, 8192, 10240, 12288, 14336, 14352, 14368, 16416
DFF = 5632
ALPHA = 2.0 ** 0.25
LAM_INIT = 0.2
C_ID, C_ONE, C_U, C_L, C_SL, C_BD, C_AB = 0, 128, 256, 384, 512, 640, 640 + 1024
NCONST = C_AB + 8 * 17


class Sch:
    NDMA = 12

    def __init__(self, nc, stack):
        self.nc = nc
        self.eng = dict(pe=nc.tensor, act=nc.scalar, dve=nc.vector, pool=nc.gpsimd, sp=nc.sync)
        self.sems = {}
        self.cnt = {}
        for e in ("pe", "act", "dve", "pool"):
            self.sems[e] = stack.enter_context(nc.semaphore("c_" + e))
            self.cnt[e] = 0
        self.dq = {}
        for q in ("sp", "pool", "act"):
            names = []
            for i in range(self.NDMA):
                n = "d_%s%d" % (q, i)
                self.sems[n] = stack.enter_context(nc.semaphore(n))
                self.cnt[n] = 0
                names.append(n)
            self.dq[q] = [names, 0]
        self.seen = {e: {} for e in self.eng}
        self.last_w = {}
        self.readers = {}
        self.n_ins = 0
        self.dead = False

    def _wait(self, e, toks):
        need = {}
        for t in toks:
            if t is None:
                continue
            if need.get(t[0], 0) < t[1]:
                need[t[0]] = t[1]
        seen = self.seen[e]
        for sem, val in need.items():
            if seen.get(sem, 0) >= val:
                continue
            self.eng[e].wait_ge(self.sems[sem], val)
            seen[sem] = val

    def _deps(self, reads, writes):
        toks = []
        for k in reads:
            toks.append(self.last_w.get(k))
        for k in writes:
            toks.append(self.last_w.get(k))
            r = self.readers.get(k)
            if r:
                toks.extend(r.items())
        return toks

    def _record(self, tok, reads, writes):
        for k in reads:
            r = self.readers.setdefault(k, {})
            if r.get(tok[0], 0) < tok[1]:
                r[tok[0]] = tok[1]
        for k in writes:
            self.last_w[k] = tok
            self.readers[k] = {}

    def op(self, e, fn, reads=(), writes=()):
        if self.dead:
            return None
        psr = [k for k in reads if isinstance(k, tuple) and k[0] == "ps" and k not in writes]
        if psr:
            writes = list(writes) + psr
        self._wait(e, self._deps(reads, writes))
        ins = fn(self.eng[e])
        self.cnt[e] += 1
        ins.then_inc(self.sems[e], 1)
        tok = (e, self.cnt[e])
        self._record(tok, reads, writes)
        self.n_ins += 1
        return tok

    def mmg(self, groups, reads=(), writes=()):
        if self.dead:
            return None
        self._wait("pe", self._deps(reads, writes))
        ins = None
        for out, pairs in groups:
            n = len(pairs)
            for i, (l, r) in enumerate(pairs):
                ins = self.nc.tensor.matmul(out, l, r, start=(i == 0), stop=(i == n - 1))
                self.n_ins += 1
        self.cnt["pe"] += 1
        ins.then_inc(self.sems["pe"], 1)
        tok = ("pe", self.cnt["pe"])
        self._record(tok, reads, writes)
        return tok

    def mm(self, out, pairs, reads=(), writes=()):
        return self.mmg([(out, pairs)], reads, writes)

    def mm1(self, out, l, r, start, stop, reads=(), writes=()):
        if self.dead:
            return None
        self._wait("pe", self._deps(reads, writes))
        ins = self.nc.tensor.matmul(out, l, r, start=start, stop=stop)
        self.cnt["pe"] += 1
        ins.then_inc(self.sems["pe"], 1)
        tok = ("pe", self.cnt["pe"])
        self._record(tok, reads, writes)
        self.n_ins += 1
        return tok

    def transposes(self, items, ident, reads=(), writes=()):
        if self.dead:
            return None
        self._wait("pe", self._deps(reads, writes))
        ins = None
        for out, in_ in items:
            k = in_.shape[0]
            ins = self.nc.tensor.transpose(out, in_, ident[:k, :k])
            self.n_ins += 1
        self.cnt["pe"] += 1
        ins.then_inc(self.sems["pe"], 1)
        tok = ("pe", self.cnt["pe"])
        self._record(tok, reads, writes)
        return tok

    def dma(self, q, out, in_, reads=(), writes=(), **kw):
        if self.dead:
            return None
        names, idx = self.dq[q]
        n = names[idx % self.NDMA]
        self.dq[q][1] = idx + 1
        prev = (n, self.cnt[n]) if self.cnt[n] else None
        self._wait(q, self._deps(reads, writes) + [prev])
        ins = self.eng[q].dma_start(out=out, in_=in_, **kw)
        self.cnt[n] += 16
        ins.then_inc(self.sems[n], 16)
        tok = (n, self.cnt[n])
        self._record(tok, reads, writes)
        self.n_ins += 1
        return tok

    def all_toks(self):
        toks = []
        for q, (names, _) in self.dq.items():
            for n in names:
                if self.cnt[n]:
                    toks.append((n, self.cnt[n]))
        for e in ("pe", "act", "dve", "pool"):
            if self.cnt[e]:
                toks.append((e, self.cnt[e]))
        return toks

    def barrier(self):
        if self.dead:
            return None
        toks = self.all_toks()
        for e in ("pe", "act", "dve", "pool", "sp"):
            self._wait(e, toks)
        self.last_w = {}
        self.readers = {}

    def finish(self):
        self._wait("sp", self.all_toks())


class Rot:
    def __init__(self, items, name):
        self.items = items
        self.name = name
        self.i = 0

    def next(self):
        j = self.i % len(self.items)
        self.i += 1
        return self.items[j], (self.name, j)


def build(dbg=False):
    nc = bass.Bass("TRN2", target_bir_lowering=False)

    def din(n, sh, dt=F32):
        return nc.dram_tensor(n, sh, dt, kind="ExternalInput").ap()

    def dout(n, sh, dt=F32):
        return nc.dram_tensor(n, sh, dt, kind="ExternalOutput").ap()

    def dscr(n, sh, dt=F32):
        return nc.dram_tensor(n, sh, dt, kind="Internal").ap()

    IN_SHAPES = {
        "x_all": [T, D], "mem": [256, D], "cache_k": [NS, PAST, 8, 256], "cache_v": [NS, PAST, 8, 256],
        "state_gdn": [NS, 16, 128, 128], "state_conv": [NS, 3, 6144], "cache_mem_k": [NS, 256, 4, 128],
        "cache_mem_v": [NS, 256, 4, 128], "w_in": [D, 18464], "conv_w": [4, 6144], "lamv": [1, 512],
        "diff_subln_g": [1, 256], "gdn_a_log": [1, 16], "gdn_dt_bias": [1, 16], "gdn_norm_g": [1, 128],
        "w_pa": [D, D], "w_pb": [D, D], "w_o": [D, D], "lnp": [6, D], "w_xq": [D, 512], "w_xk": [D, 512],
        "w_xv": [D, 512], "w_xo": [512, D], "w_ff1": [D, DFF], "w_ff3": [D, DFF], "w_ff2": [DFF, D],
        "consts": [128, NCONST], "ident_bf": [128, 128],
    }
    used_in = {}

    def gin(n):
        if n not in used_in:
            used_in[n] = din(n, IN_SHAPES[n], BF16 if n == "ident_bf" else F32)
        return used_in[n]

    build.used_in = used_in

    y_o = dout("y", [T, D])
    dk_o = dout("dk", [T, D])
    dv_o = dout("dv", [T, D])
    gs_o = dout("gstate", [1 + NS, 16, 128, 128])
    gc_o = dout("gconv", [1 + NS, 3, 6144])
    mk_o = dout("memk", [256, 512])
    mv_o = dout("memv", [256, 512])

    oaT_d = (dout if dbg else dscr)("oaT", [D, T], BF16)
    obT_d = (dout if dbg else dscr)("obT", [D, T], BF16)
    sgT_d = dscr("sgT", [2 * D, T], BF16)
    yp_d = dscr("yps", [T, D])
    h1_d = (dout if dbg else dscr)("h1s", [T, D])
    h2_d = (dout if dbg else dscr)("h2s", [T, D])

    def w_in_v():
        return gin("w_in").rearrange("(kc p) n -> p kc n", p=128)

    with contextlib.ExitStack() as top:
        S = Sch(nc, top)

        uniq = [0]

        def sbuf(st, n, sh, dt=F32):
            uniq[0] += 1
            return st.enter_context(nc.sbuf_tensor("%s_u%d" % (n, uniq[0]), sh, dt))

        ps = [top.enter_context(nc.psum_tensor("ps%d" % i, [128, 512], F32)) for i in range(8)]

        def psrot(idx, name):
            return Rot([ps[i] for i in idx], name)

        cst = sbuf(top, "cst", [128, NCONST])
        identbf = sbuf(top, "identbf", [128, 128], BF16)
        lamt = sbuf(top, "lamt", [128, 4])
        memKT = sbuf(top, "memKT", [128, 1 + NS, 4, 256], BF16)
        memV = sbuf(top, "memV", [128, (1 + NS) * 8, 130], BF16)
        xscope = contextlib.ExitStack()
        xT = sbuf(xscope, "xT", [128, 16, T], BF16)
        ident = cst[:, C_ID:C_ID + 128]
        ones = cst[:, C_ONE:C_ONE + 128]
        Umask = cst[:, C_U:C_U + 128]
        SLmask = cst[:, C_SL:C_SL + 128]

        stop = (build.phases or {}).get("stop", 99)

        class StopBuild(Exception):
            pass

        def gate(k):
            if stop <= k:
                S.dead = True

        try:
          with contextlib.ExitStack() as st:
              S.dma("sp", cst[:], gin("consts")[:, :], writes=["cst"])
              S.dma("sp", identbf[:], gin("ident_bf")[:, :], writes=["identbf"])
              lv = sbuf(st, "lv", [128, 4, 128])
              S.dma("sp", lv[:].rearrange("p a b -> p (a b)"), gin("lamv")[0:1, :].partition_broadcast(128), writes=["lv"])
              lp = sbuf(st, "lp", [128, 2, 128])
              S.op("dve", lambda e: e.tensor_tensor(out=lp[:], in0=lv[:, 0:4:2, :], in1=lv[:, 1:4:2, :], op=ALU.mult),
                   reads=["lv"], writes=["lp"])
              ls = sbuf(st, "ls", [128, 2])
              S.op("dve", lambda e: e.reduce_sum(out=ls[:], in_=lp[:], axis=mybir.AxisListType.X), reads=["lp"], writes=["ls"])
              le = sbuf(st, "le", [128, 2])
              S.op("act", lambda e: e.activation(out=le[:], in_=ls[:], func=AF.Exp), reads=["ls"], writes=["le"])
              S.op("dve", lambda e: e.tensor_tensor(out=lamt[:, 0:1], in0=le[:, 0:1], in1=le[:, 1:2], op=ALU.subtract),
                   reads=["le"], writes=["lamt"])
              S.op("dve", lambda e: e.tensor_scalar(out=lamt[:, 0:1], in0=lamt[:, 0:1], scalar1=LAM_INIT, scalar2=None,
                                                    op0=ALU.add), reads=["lamt"], writes=["lamt"])
              S.op("dve", lambda e: e.tensor_scalar(out=lamt[:, 1:2], in0=lamt[:, 0:1], scalar1=-1.0, scalar2=None,
                                                    op0=ALU.mult), reads=["lamt"], writes=["lamt"])
              gate(1)
              xs = Rot([sbuf(st, "xs%d" % i, [128, D]) for i in range(2)], "xs")
              pr = psrot([1, 2, 3, 4, 5, 6, 7], "ps")
              ei = 0

              def load_T(src_rows, nt, dst, t0):
                  nonlocal ei
                  b, bk = xs.next()
                  S.dma("sp", b[:nt, :], src_rows, writes=[bk])
                  for g in range(4):
                      p, pk = pr.next()
                      S.transposes([(p[:, j * 128:j * 128 + nt], b[:nt, (4 * g + j) * 128:(4 * g + j + 1) * 128])
                                    for j in range(4)], ident, reads=[bk, "cst"], writes=[pk])
                      src = p[:].rearrange("p (a b) -> p a b", a=4)[:, :, :nt]
                      dd = dst[:, 4 * g:4 * g + 4, t0:t0 + nt]
                      ei += 1
                      if ei % 2:
                          S.op("act", lambda e: e.copy(out=dd, in_=src), reads=[pk], writes=[("T", ei)])
                      else:
                          S.op("dve", lambda e: e.tensor_copy(out=dd, in_=src), reads=[pk], writes=[("T", ei)])

              for (t0, nt) in BLOCKS:
                  load_T(gin("x_all")[t0:t0 + nt, :], nt, xT, t0)
              memT = sbuf(st, "memT", [128, 16, 256], BF16)
              for mb in range(2):
                  load_T(gin("mem")[mb * 128:(mb + 1) * 128, :], 128, memT, mb * 128)
              S.barrier()
              gate(2)
              wx = sbuf(st, "wx", [128, 16, 1024], BF16)
              S.dma("pool", wx[:, :, 0:512], gin("w_xk").rearrange("(kc p) n -> p kc n", p=128), writes=["wx"])
              S.dma("pool", wx[:, :, 512:1024], gin("w_xv").rearrange("(kc p) n -> p kc n", p=128), writes=["wx"])
              mst = Rot([sbuf(st, "mst%d" % i, [128, 512]) for i in range(2)], "mst")
              dbgt = sbuf(st, "dbgt", [128, 4, 128])
              gate(2.2)
              S.op("dve", lambda e: e.memset(memV[:], 1.0), writes=["memV"])
              gate(2.4)
              for mb in range(2):
                  for kv in range(2):
                      p, pk = pr.next()
                      S.mm(p[:, :], [(memT[:, kc, mb * 128:(mb + 1) * 128], wx[:, kc, kv * 512:(kv + 1) * 512])
                                     for kc in range(16)], reads=["wx"], writes=[pk])
                      m, mk = mst.next()
                      S.op("act", lambda e: e.copy(out=m[:], in_=p[:]), reads=[pk], writes=[mk])
                      S.dma("sp", (mk_o, mv_o)[kv][mb * 128:(mb + 1) * 128, :], m[:], reads=[mk])
                      if kv == 1:
                          S.op("dve", lambda e: e.tensor_copy(out=memV[:, mb * 4:mb * 4 + 4, 0:128],
                                                              in_=p[:].rearrange("p (h d) -> p h d", h=4)),
                               reads=[pk], writes=["memV"])
              gate(2.5)
              gate(2.6)
              for hd in range(4):
                  p, pk = pr.next()
                  S.mm(p[:, 0:256], [(wx[:, kc, hd * 128:(hd + 1) * 128], memT[:, kc, :]) for kc in range(16)],
                       reads=["wx"], writes=[pk])
                  S.op("act", lambda e: e.copy(out=memKT[:, 0, hd, :], in_=p[:, 0:256]), reads=[pk], writes=["memKT"])
              gate(3)
              cms = sbuf(st, "cms", [128, 2, 512])
              cmb = sbuf(st, "cmb", [128, 2, 512], BF16)
              for s in range(NS):
                  S.dma("sp", cms[:], gin("cache_mem_k")[s].rearrange("(mb p) h d -> p mb (h d)", p=128), writes=["cms"])
                  S.op("dve", lambda e: e.tensor_copy(out=cmb[:], in_=cms[:]), reads=["cms"], writes=["cmb"])
                  p, pk = pr.next()
                  pb = p[:].bitcast(BF16)
                  S.transposes([(pb[:, (hd * 2 + mb) * 128:(hd * 2 + mb + 1) * 128], cmb[:, mb, hd * 128:(hd + 1) * 128])
                                for hd in range(4) for mb in range(2)], identbf, reads=["cmb", "identbf"], writes=[pk])
                  S.op("act", lambda e: e.copy(out=memKT[:, 1 + s, :, :], in_=pb[:, 0:1024].rearrange("p (h m) -> p h m", h=4)),
                       reads=[pk], writes=["memKT"])
                  S.dma("sp", cms[:], gin("cache_mem_v")[s].rearrange("(mb p) h d -> p mb (h d)", p=128), reads=[], writes=["cms"])
                  for mb in range(2):
                      S.op("dve", lambda e: e.tensor_copy(out=memV[:, ((1 + s) * 2 + mb) * 4:((1 + s) * 2 + mb) * 4 + 4, 0:128],
                                                          in_=cms[:, mb, :].rearrange("p (h d) -> p h d", h=4)),
                           reads=["cms"], writes=["memV"])
              S.barrier()
        except StopBuild:
            pass
        S.dead = False

        PH = dict(attn=True, gdn=True, post=True)
        PH.update(build.phases or {})

        if PH["attn"]:
            with contextlib.ExitStack() as st:
                WQ = Rot([sbuf(st, "wq%d" % i, [128, 16, 256], BF16) for i in range(2)], "wq")
                WKV = Rot([sbuf(st, "wkv%d" % i, [128, 16, 512], BF16) for i in range(2)], "wkv")
                qT = sbuf(st, "qT", [128, 2, T], BF16)
                kT = sbuf(st, "kT", [128, 2, T], BF16)
                Vaug = sbuf(st, "Vaug", [128, NB, 258], BF16)
                oaT = Rot([sbuf(st, "oaT%d" % i, [128, 2, T], BF16) for i in range(2)], "oaT")
                KVS = Rot([sbuf(st, "kvs%d" % i, [128, 512]) for i in range(3)], "kvs")
                ETr = Rot([sbuf(st, "ET%d" % i, [128, 2, 128], BF16) for i in range(4)], "ET")
                TMPr = Rot([sbuf(st, "tmpd%d" % i, [128, 2, 128]) for i in range(2)], "tmpd")
                subg = sbuf(st, "subg", [128, 256])
                ckb = Rot([sbuf(st, "ckb%d" % i, [128, 8, 256], BF16) for i in range(1)], "ckb")
                kTc = Rot([sbuf(st, "kTc%d" % i, [128, 2, PAST], BF16) for i in range(1)], "kTc")
                Vc = Rot([sbuf(st, "Vc%d" % i, [128, 8, 258], BF16) for i in range(1)], "Vc")
                o1r = Rot([sbuf(st, "o1_%d" % i, [128, 256]) for i in range(2)], "o1")
                o2r = Rot([sbuf(st, "o2_%d" % i, [128, 256]) for i in range(2)], "o2")
                jnk = sbuf(st, "jnk", [128, 256])
                onr = Rot([sbuf(st, "on_%d" % i, [128, 256], BF16) for i in range(2)], "on")
                smr = Rot([sbuf(st, "sm_%d" % i, [128, 8]) for i in range(3)], "sm")

                S.dma("sp", subg[:], gin("diff_subln_g")[0:1, :].partition_broadcast(128), writes=["subg"])
                S.op("dve", lambda e: e.tensor_scalar(out=subg[:], in0=subg[:], scalar1=1.0 - LAM_INIT, scalar2=None,
                                                      op0=ALU.mult), reads=["subg"], writes=["subg"])
                S.op("dve", lambda e: e.memset(Vaug[:], 1.0), writes=["Vaug"])
                for i in range(1):
                    S.op("dve", lambda e: e.memset(Vc.items[i][:], 1.0), writes=[("Vc", i)])
                ppj = psrot([0, 1, 2], "ps")
                pO = [[(ps[3], ("ps", 3)), (ps[4], ("ps", 4))], [(ps[5], ("ps", 5)), (ps[6], ("ps", 6))]]
                pT = (ps[7], ("ps", 7))
                fin = 0
                for h in range(int((build.phases or {}).get('nh', 8))):
                    wq, wqk = WQ.next()
                    wkv, wkvk = WKV.next()
                    S.dma("pool", wq[:], w_in_v()[:, :, DQ + h * 256:DQ + (h + 1) * 256], writes=[wqk])
                    S.dma("pool", wkv[:, :, 0:256], w_in_v()[:, :, DK + h * 256:DK + (h + 1) * 256], writes=[wkvk])
                    S.dma("pool", wkv[:, :, 256:512], w_in_v()[:, :, DV + h * 256:DV + (h + 1) * 256], writes=[wkvk])
                    for m in range(2):
                        for (s0, sn) in SPANS:
                            p, pk = ppj.next()
                            S.mm(p[:, :sn], [(wq[:, kc, m * 128:(m + 1) * 128], xT[:, kc, s0:s0 + sn]) for kc in range(16)],
                                 reads=[wqk], writes=[pk])
                            S.op("act", lambda e: e.activation(out=qT[:, m, s0:s0 + sn], in_=p[:, :sn], func=AF.Copy,
                                                               scale=128.0 ** -0.5), reads=[pk], writes=["qT"])
                            p, pk = ppj.next()
                            S.mm(p[:, :sn], [(wkv[:, kc, m * 128:(m + 1) * 128], xT[:, kc, s0:s0 + sn]) for kc in range(16)],
                                 reads=[wkvk], writes=[pk])
                            S.op("dve", lambda e: e.tensor_copy(out=kT[:, m, s0:s0 + sn], in_=p[:, :sn]),
                                 reads=[pk], writes=["kT"])
                    for bi, (t0, nt) in enumerate(BLOCKS):
                        p, pk = ppj.next()
                        S.mm(p[:nt, :], [(xT[:, kc, t0:t0 + nt], wkv[:, kc, :]) for kc in range(16)],
                             reads=[wkvk], writes=[pk])
                        kv, kvk = KVS.next()
                        S.op("act", lambda e: e.copy(out=kv[:nt, :], in_=p[:nt, :]), reads=[pk], writes=[kvk])
                        S.op("dve", lambda e: e.tensor_copy(out=Vaug[:nt, bi, 0:256], in_=p[:nt, 256:512]),
                             reads=[pk], writes=["Vaug"])
                        S.dma("sp", dk_o[t0:t0 + nt, h * 256:(h + 1) * 256], kv[:nt, 0:256], reads=[kvk])
                        S.dma("sp", dv_o[t0:t0 + nt, h * 256:(h + 1) * 256], kv[:nt, 256:512], reads=[kvk])
                    oa, oak = oaT.next()
                    deferred = []
                    for qi, (q0, nq) in enumerate(BLOCKS):
                        kbl = []
                        if qi < 16:
                            for kb in range(qi + 1):
                                kbl.append((kT[:, :, kb * 128:(kb + 1) * 128], Vaug[:, kb, 0:257], 128,
                                            ("v", qi - kb) if kb < qi else ("m",), ["kT", "Vaug"]))
                        else:
                            s = qi - 16
                            cb, cbk = ckb.next()
                            S.dma("pool", cb[:], gin("cache_k")[s, :, h, :].rearrange("(kb p) d -> p kb d", p=128), writes=[cbk])
                            kc_, kck = kTc.next()
                            for m in range(2):
                                tp, tpk = ppj.next()
                                tpb = tp[:].bitcast(BF16)
                                S.transposes([(tpb[:, kb * 128:(kb + 1) * 128], cb[:, kb, m * 128:(m + 1) * 128])
                                              for kb in range(8)], identbf, reads=[cbk, "identbf"], writes=[tpk])
                                S.op("act", lambda e: e.copy(out=kc_[:, m, :], in_=tpb[:, 0:1024]), reads=[tpk], writes=[kck])
                            vc, vck = Vc.next()
                            S.dma("pool", vc[:, :, 0:256], gin("cache_v")[s, :, h, :].rearrange("(kb p) d -> p kb d", p=128),
                                  writes=[vck])
                            for kb in range(8):
                                kbl.append((kc_[:, :, kb * 128:(kb + 1) * 128], vc[:, kb, 0:257], 128, ("v", 8 - kb), [kck, vck]))
                            kbl.append((kT[:, :, q0:q0 + nq], Vaug[:, qi, 0:257], nq, ("m",), ["kT", "Vaug"]))
                        (O0, O0k), (O1, O1k) = pO[qi % 2]
                        nkb = len(kbl)
                        def qk_issue(ki):
                            ksrc, vsrc, nk, bspec, kkeys = kbl[ki]
                            stp, stk = ppj.next()
                            S.mmg([(stp[:nk, m * 128:m * 128 + nq], [(ksrc[:, m, :], qT[:, m, q0:q0 + nq])]) for m in range(2)],
                                  reads=["qT"] + kkeys, writes=[stk])
                            return stp, stk
                        pend = qk_issue(0)
                        for ki, (ksrc, vsrc, nk, bspec, kkeys) in enumerate(kbl):
                            stp, stk = pend
                            if ki + 1 < nkb:
                                pend = qk_issue(ki + 1)
                            stv = stp[:nk, 0:256].rearrange("p (m q) -> p m q", m=2)[:, :, :nq]
                            et, etk = ETr.next()
                            if bspec[0] == "v":
                                col = C_AB + h * 17 + bspec[1]
                                S.op("act", lambda e: e.activation(out=et[:nk, :, :nq], in_=stv, func=AF.Exp,
                                                                   bias=cst[:nk, col:col + 1], scale=1.0),
                                     reads=[stk], writes=[etk])
                            else:
                                tm, tmk = TMPr.next()
                                bd = cst[:nk, C_BD + h * 128:C_BD + h * 128 + nq].unsqueeze(1).broadcast_to([nk, 2, nq])
                                S.op("dve", lambda e: e.tensor_tensor(out=tm[:nk, :, :nq], in0=stv, in1=bd, op=ALU.add),
                                     reads=[stk], writes=[tmk])
                                S.op("act", lambda e: e.activation(out=et[:nk, :, :nq], in_=tm[:nk, :, :nq], func=AF.Exp),
                                     reads=[tmk], writes=[etk])
                            first, last = ki == 0, ki == nkb - 1
                            for m, (O, Ok) in enumerate(((O0, O0k), (O1, O1k))):
                                S.mm1(O[:nq, 0:257], et[:nk, m, :nq], vsrc[:nk, :], first, last,
                                      reads=[etk] + kkeys, writes=([Ok] if (first or last) else []))
                        sm, smk = smr.next()
                        S.op("dve", lambda e: e.reciprocal(out=sm[:nq, 0:1], in_=O0[:nq, 256:257]), reads=[O0k], writes=[smk])
                        S.op("dve", lambda e: e.reciprocal(out=sm[:nq, 1:2], in_=O1[:nq, 256:257]), reads=[O1k], writes=[smk])
                        S.op("dve", lambda e: e.tensor_tensor(out=sm[:nq, 2:3], in0=sm[:nq, 1:2], in1=lamt[:nq, 1:2],
                                                              op=ALU.mult), reads=[smk], writes=[smk])
                        o1, o1k = o1r.next()
                        S.op("act", lambda e: e.activation(out=o1[:nq, :], in_=O0[:nq, 0:256], func=AF.Copy,
                                                           scale=sm[:nq, 0:1]), reads=[O0k, smk], writes=[o1k])
                        o2, o2k = o2r.next()
                        S.op("dve", lambda e: e.scalar_tensor_tensor(out=o2[:nq, :], in0=O1[:nq, 0:256], scalar=sm[:nq, 2:3],
                                                                     in1=o1[:nq, :], op0=ALU.mult, op1=ALU.add),
                             reads=[O1k, o1k, smk], writes=[o2k])
                        S.op("act", lambda e: e.activation(out=jnk[:nq, :], in_=o2[:nq, :], func=AF.Square,
                                                           accum_out=sm[:nq, 3:4]), reads=[o2k, smk], writes=["jnk", smk])
                        S.op("act", lambda e: e.activation(out=sm[:nq, 4:5], in_=sm[:nq, 3:4], func=AF.Sqrt,
                                                           scale=1.0 / 256, bias=1e-6), reads=[smk], writes=[smk])
                        S.op("dve", lambda e: e.reciprocal(out=sm[:nq, 5:6], in_=sm[:nq, 4:5]), reads=[smk], writes=[smk])
                        on, onk = onr.next()
                        S.op("dve", lambda e: e.scalar_tensor_tensor(out=on[:nq, :], in0=o2[:nq, :], scalar=sm[:nq, 5:6],
                                                                     in1=subg[:nq, :], op0=ALU.mult, op1=ALU.mult),
                             reads=[o2k, smk, "subg"], writes=[onk])
                        def fin_tr(on=on, onk=onk, q0=q0, nq=nq, oa=oa, oak=oak):
                            tpb = pT[0][:].bitcast(BF16)
                            S.transposes([(tpb[:, c * 128:c * 128 + nq], on[:nq, c * 128:(c + 1) * 128]) for c in range(2)],
                                         identbf, reads=[onk, "identbf"], writes=[pT[1]])
                            src = tpb[:, 0:256].rearrange("p (c q) -> p c q", c=2)[:, :, :nq]
                            S.op("act", lambda e: e.copy(out=oa[:, :, q0:q0 + nq], in_=src), reads=[pT[1]], writes=[oak])
                        deferred.append(fin_tr)
                        if len(deferred) > 1:
                            deferred.pop(0)()
                    while deferred:
                        deferred.pop(0)()
                    S.dma("sp", oaT_d[h * 256:(h + 1) * 256, :].rearrange("(c p) t -> p c t", p=128), oa[:], reads=[oak])
                S.barrier()


        if PH["gdn"]:
            with contextlib.ExitStack() as st:
                NGH = int((build.phases or {}).get("ngh", 16))
                TP = 3 + NPR + NS * (3 + TS)
                NCV = TP - 3
                SEGS = [(0, 0, NPR)] + [(NPR + 3 + (3 + TS) * s_, NPR + TS * s_, TS) for s_ in range(NS)]
                CSP = [(i, min(512, NCV - i)) for i in range(0, NCV, 512)]
                alog_b = sbuf(st, "alog_b", [128, 16])
                dtb_b = sbuf(st, "dtb_b", [128, 16])
                gng_b = sbuf(st, "gng_b", [128, 128])
                S.dma("sp", alog_b[:], gin("gdn_a_log")[0:1, :].partition_broadcast(128), writes=["alog"])
                S.dma("sp", dtb_b[:], gin("gdn_dt_bias")[0:1, :].partition_broadcast(128), writes=["dtb"])
                S.dma("sp", gng_b[:], gin("gdn_norm_g")[0:1, :].partition_broadcast(128), writes=["gng"])
                S.op("act", lambda e: e.activation(out=alog_b[:], in_=alog_b[:], func=AF.Exp), reads=["alog"], writes=["alog"])
                S.op("dve", lambda e: e.tensor_scalar(out=alog_b[:], in0=alog_b[:], scalar1=-1.0, scalar2=None, op0=ALU.mult),
                     reads=["alog"], writes=["alog"])
                wab = sbuf(st, "wab", [128, 16, 32], BF16)
                S.dma("pool", wab[:], w_in_v()[:, :, GA:GA + 32], writes=["wab"])
                gall = sbuf(st, "gall", [128, NB, 16])
                ball = sbuf(st, "ball", [128, NB, 16])
                nball = sbuf(st, "nball", [128, NB, 16])
                gcall = sbuf(st, "gcall", [128, NB, 16])
                egc = sbuf(st, "egc", [128, NB, 16])
                bgc = sbuf(st, "bgc", [128, NB, 16])
                t16 = Rot([sbuf(st, "t16_%d" % i, [128, 16]) for i in range(2)], "t16")
                for tl in (gall, ball, gcall):
                    S.op("dve", lambda e: e.memset(tl[:], 0.0), writes=["gates"])
                pg = psrot([0, 1, 2, 3], "ps")
                for bi, (t0, nt) in enumerate(BLOCKS):
                    p, pk = pg.next()
                    S.mm(p[:nt, 0:32], [(xT[:, kc, t0:t0 + nt], wab[:, kc, :]) for kc in range(16)], reads=["wab"], writes=[pk])
                    ta, tak = t16.next()
                    S.op("dve", lambda e: e.tensor_tensor(out=ta[:nt, :], in0=p[:nt, 0:16], in1=dtb_b[:nt, :], op=ALU.add),
                         reads=[pk, "dtb"], writes=[tak])
                    S.op("act", lambda e: e.activation(out=ball[:nt, bi, :], in_=p[:nt, 16:32], func=AF.Sigmoid),
                         reads=[pk, "gates"], writes=[("ball", bi)])
                    S.op("act", lambda e: e.activation(out=ta[:nt, :], in_=ta[:nt, :], func=AF.Exp), reads=[tak], writes=[tak])
                    S.op("act", lambda e: e.activation(out=ta[:nt, :], in_=ta[:nt, :], func=AF.Ln, bias=1.0), reads=[tak], writes=[tak])
                    S.op("dve", lambda e: e.tensor_tensor(out=gall[:nt, bi, :], in0=ta[:nt, :], in1=alog_b[:nt, :], op=ALU.mult),
                         reads=[tak, "alog", "gates"], writes=[("gall", bi)])
                    p2, pk2 = pg.next()
                    S.mm(p2[:nt, 0:16], [(Umask[:nt, :nt], gall[:nt, bi, :])], reads=[("gall", bi)], writes=[pk2])
                    S.op("act", lambda e: e.copy(out=gcall[:nt, bi, :], in_=p2[:nt, 0:16]), reads=[pk2, "gates"], writes=[("gc", bi)])
                S.barrier()
                S.op("act", lambda e: e.activation(out=egc[:], in_=gcall[:], func=AF.Exp), writes=["egc"])
                S.op("dve", lambda e: e.tensor_scalar(out=nball[:], in0=ball[:], scalar1=-1.0, scalar2=None, op0=ALU.mult), writes=["nball"])
                S.op("dve", lambda e: e.tensor_tensor(out=bgc[:], in0=ball[:], in1=egc[:], op=ALU.mult), reads=["egc"], writes=["bgc"])
                S.barrier()

                HG = int((build.phases or {}).get("hg", 3))
                heads = []
                for j in range(HG):
                    heads.append(dict(
                        qhT=sbuf(st, "qhT%d" % j, [128, T], BF16), khT=sbuf(st, "khT%d" % j, [128, T], BF16),
                        vsT=sbuf(st, "vsT%d" % j, [128, T], BF16),
                        S=sbuf(st, "S%d" % j, [128, 128]), Sbf=sbuf(st, "Sbf%d" % j, [128, 128], BF16),
                        obT=sbuf(st, "obT%d" % j, [128, T], BF16), j=j))
                identr = sbuf(st, "identr", [128, 128], F32R)
                S.op("act", lambda e: e.copy(out=identr[:], in_=ident), writes=["identr"])
                NEUM = int((build.phases or {}).get("neum", 2))
                NEU_DT = {0: F32, 1: BF16, 2: F32R}[NEUM]
                G = {}

                def alloc_proj(sc):
                    G["WG"] = Rot([sbuf(sc, "wg%d" % i, [128, 16, 128], BF16) for i in range(2)], "wg")
                    G["pre"] = sbuf(sc, "pre", [128, TP])
                    G["cva"] = sbuf(sc, "cva", [128, NCV])
                    G["cvb"] = sbuf(sc, "cvb", [128, NCV])
                    G["cwt"] = Rot([sbuf(sc, "cw%d" % i, [128, 4]) for i in range(2)], "cw")
                    S.op("dve", lambda e: e.memset(G["pre"][:], 0.0), writes=["pre"])

                def alloc_blk(sc):
                    TMP_ = []
                    for j in range(HG):
                        d = {}
                        for n in ("diag", "t1", "t2", "eT", "e", "decT", "dec", "oA", "o", "jk", "PT"):
                            d[n] = sbuf(sc, "g_%s%d" % (n, j), [128, 128])
                        for n in ("X", "XT", "X2", "XT2", "PTm"):
                            d[n] = sbuf(sc, "g_%s%d" % (n, j), [128, 128], NEU_DT)
                        for n in ("ktm", "vb", "kbg", "PTbf", "vnew", "ob"):
                            d[n] = sbuf(sc, "g_%s%d" % (n, j), [128, 128], BF16)
                        TMP_.append(d)
                    G["TMP"] = TMP_
                    G["ZG"] = Rot([sbuf(sc, "zg2_%d" % i, [128, HG * 128]) for i in range(2)], "zg2")
                    G["WZ2"] = sbuf(sc, "wz2", [128, 16, HG * 128], BF16)
                    X_ = {}
                    for j in range(HG):
                        for par in range(2):
                            d = {}
                            d["u"] = sbuf(sc, "x_u%d%d" % (j, par), [128, 128])
                            for n in ("kd", "qkT", "wT"):
                                d[n] = sbuf(sc, "x_%s%d%d" % (n, j, par), [128, 128], BF16)
                            d["sm"] = sbuf(sc, "x_sm%d%d" % (j, par), [128, 8])
                            X_[(j, par)] = d
                    G["XCH"] = X_

                def gdn_proj(h, hs):
                    j = hs["j"]
                    for ci, cbase in enumerate((GQ, GK, GV)):
                        col0 = cbase + h * 128
                        ccol = ci * 2048 + h * 128
                        wg, wgk = G["WG"].next()
                        S.dma("pool", wg[:], w_in_v()[:, :, col0:col0 + 128], writes=[wgk])
                        cw, cwk = G["cwt"].next()
                        S.dma("sp", cw[:], gin("conv_w")[:, ccol:ccol + 128].rearrange("j c -> c j"), writes=[cwk],
                              allow_slow_non_contiguous=True)
                        for s_ in range(NS):
                            i0 = SEGS[1 + s_][0]
                            S.dma("sp", G["pre"][:, i0:i0 + 3], gin("state_conv")[s_, :, ccol:ccol + 128].rearrange("j c -> c j"),
                                  writes=["pre"], allow_slow_non_contiguous=True)
                        for (s0, sn) in SPANS:
                            p, pk = pg.next()
                            S.mm(p[:, :sn], [(wg[:, kc, :], xT[:, kc, s0:s0 + sn]) for kc in range(16)], reads=[wgk], writes=[pk])
                            if s0 < NPR:
                                S.op("act", lambda e: e.copy(out=G["pre"][:, 3 + s0:3 + s0 + sn], in_=p[:, :sn]), reads=[pk], writes=["pre"])
                            else:
                                for s_ in range(NS):
                                    i0 = SEGS[1 + s_][0] + 3
                                    S.op("act", lambda e: e.copy(out=G["pre"][:, i0:i0 + TS], in_=p[:, TS * s_:TS * (s_ + 1)]),
                                         reads=[pk], writes=["pre"])
                        S.dma("sp", gc_o[0, :, ccol:ccol + 128].rearrange("j c -> c j"), G["pre"][:, NPR:NPR + 3], reads=["pre"],
                              allow_slow_non_contiguous=True)
                        for s_ in range(NS):
                            i0 = SEGS[1 + s_][0] + TS
                            S.dma("sp", gc_o[1 + s_, :, ccol:ccol + 128].rearrange("j c -> c j"), G["pre"][:, i0:i0 + 3], reads=["pre"],
                                  allow_slow_non_contiguous=True)
                        S.op("dve", lambda e: e.tensor_scalar(out=G["cva"][:], in0=G["pre"][:, 0:NCV], scalar1=cw[:, 0:1], scalar2=None,
                                                              op0=ALU.mult), reads=["pre", cwk], writes=["cva"])
                        for jj in range(1, 4):
                            S.op("dve", lambda e: e.scalar_tensor_tensor(out=G["cva"][:], in0=G["pre"][:, jj:jj + NCV], scalar=cw[:, jj:jj + 1],
                                                                         in1=G["cva"][:], op0=ALU.mult, op1=ALU.add),
                                 reads=["pre", cwk, "cva"], writes=["cva"])
                        if ci == 2:
                            for (i0, tk0, n) in SEGS:
                                S.op("act", lambda e: e.activation(out=hs["vsT"][:, tk0:tk0 + n], in_=G["cva"][:, i0:i0 + n], func=AF.Silu),
                                     reads=["cva"], writes=[("vsT", j)])
                            continue
                        S.op("act", lambda e: e.activation(out=G["cva"][:], in_=G["cva"][:], func=AF.Silu), reads=["cva"], writes=["cva"])
                        S.op("dve", lambda e: e.tensor_tensor(out=G["cvb"][:], in0=G["cva"][:], in1=G["cva"][:], op=ALU.mult), reads=["cva"], writes=["cvb"])
                        for (i0, n) in CSP:
                            p, pk = pg.next()
                            S.mm(p[:, :n], [(ones, G["cvb"][:, i0:i0 + n])], reads=["cvb"], writes=[pk])
                            S.op("act", lambda e: e.activation(out=G["pre"][:, i0:i0 + n], in_=p[:, :n], func=AF.Sqrt, bias=1e-6),
                                 reads=[pk, "cva"], writes=["pre"])
                        S.op("dve", lambda e: e.reciprocal(out=G["pre"][:, 0:NCV], in_=G["pre"][:, 0:NCV]), reads=["pre"], writes=["pre"])
                        dst = hs["qhT"] if ci == 0 else hs["khT"]
                        sc = 128.0 ** -0.5 if ci == 0 else 1.0
                        for (i0, tk0, n) in SEGS:
                            S.op("dve", lambda e: e.scalar_tensor_tensor(out=dst[:, tk0:tk0 + n], in0=G["cva"][:, i0:i0 + n], scalar=sc,
                                                                         in1=G["pre"][:, i0:i0 + n], op0=ALU.mult, op1=ALU.mult),
                                 reads=["cva", "pre"], writes=[("qk", j, ci)])
                        S.op("dve", lambda e: e.memset(G["pre"][:, 0:3], 0.0), reads=[("qk", j, ci)], writes=["pre"])

                def gen_gz(bi, t0, nt, zg, zgk, nh_):
                    pZ, pZk = ps[7], ("ps", 7)
                    nc_ = nh_ * 128
                    S.mm(pZ[:nt, 0:nc_], [(xT[:, kc, t0:t0 + nt], G["WZ2"][:, kc, 0:nc_]) for kc in range(16)], reads=["wz2"], writes=[pZk])
                    S.op("act", lambda e: e.activation(out=zg[:nt, 0:nc_], in_=pZ[:nt, 0:nc_], func=AF.Silu), reads=[pZk], writes=[zgk])
                    yield
                    S.op("dve", lambda e: e.tensor_tensor(out=zg[:nt, 0:nc_].rearrange("p (a b) -> p a b", a=nh_),
                                                          in0=zg[:nt, 0:nc_].rearrange("p (a b) -> p a b", a=nh_),
                                                          in1=gng_b[:nt, :].unsqueeze(1).broadcast_to([nt, nh_, 128]), op=ALU.mult),
                         reads=[zgk, "gng"], writes=[zgk])
                    yield

                def gen_solve(h, hs, bi, t0, nt):
                    j = hs["j"]
                    tm = G["TMP"][j]
                    xc = G["XCH"][(j, bi % 2)]
                    k = lambda n: ("g", n, j)
                    kx = lambda n: ("x", n, j, bi % 2)
                    psl = [(ps[2 * j + i], ("ps", 2 * j + i)) for i in range(2)]
                    cnt = [0]

                    def nps():
                        cnt[0] += 1
                        return psl[cnt[0] % 2]
                    blk = slice(t0, t0 + nt)
                    gcol = gcall[:nt, bi, h:h + 1]
                    hk = [("qk", j, 0), ("qk", j, 1), ("vsT", j)]
                    sm = xc["sm"]
                    tp, tpk = nps()
                    tpb = tp[:].bitcast(BF16)
                    S.transposes([(tpb[:nt, 0:128], hs["khT"][:, blk]), (tpb[:nt, 128:256], hs["vsT"][:, blk])], identbf,
                                 reads=hk + ["identbf"], writes=[tpk])
                    S.op("act", lambda e: e.copy(out=tm["ktm"][:nt, :], in_=tpb[:nt, 0:128]), reads=[tpk], writes=[k("ktm")])
                    S.op("dve", lambda e: e.tensor_scalar(out=tm["vb"][:nt, :], in0=tpb[:nt, 128:256], scalar1=ball[:nt, bi, h:h + 1],
                                                          scalar2=None, op0=ALU.mult), reads=[tpk], writes=[k("vb")])
                    S.op("dve", lambda e: e.tensor_scalar(out=tm["kbg"][:nt, :], in0=tpb[:nt, 0:128], scalar1=bgc[:nt, bi, h:h + 1],
                                                          scalar2=None, op0=ALU.mult), reads=[tpk], writes=[k("kbg")])
                    S.op("dve", lambda e: e.tensor_scalar(out=tm["diag"][:nt, :nt], in0=ident[:nt, :nt], scalar1=gcol, scalar2=None,
                                                          op0=ALU.mult), writes=[k("diag")])
                    yield
                    pB, pBk = nps()
                    pK, pKk = nps()
                    S.mm(pB[:, :nt], [(ones[:nt, :], tm["diag"][:nt, :nt])], reads=[k("diag")], writes=[pBk])
                    S.mmg([(pK[:nt, 0:nt], [(hs["khT"][:, blk], hs["khT"][:, blk])]),
                           (pK[:nt, 128:128 + nt], [(hs["khT"][:, blk], hs["qhT"][:, blk])])], reads=hk, writes=[pKk])
                    yield
                    S.op("dve", lambda e: e.tensor_scalar(out=tm["t1"][:nt, :nt], in0=pB[:nt, :nt], scalar1=gcol, scalar2=0.0,
                                                          op0=ALU.subtract, op1=ALU.min), reads=[pBk], writes=[k("t1")])
                    S.op("dve", lambda e: e.tensor_scalar(out=tm["t2"][:nt, :nt], in0=pB[:nt, :nt], scalar1=gcol, scalar2=0.0,
                                                          op0=ALU.subtract, op1=ALU.max), reads=[pBk], writes=[k("t2")])
                    S.op("act", lambda e: e.copy(out=sm[:, 0:1], in_=pB[:, nt - 1:nt]), reads=[pBk], writes=[kx("sm0")])
                    yield
                    S.op("act", lambda e: e.activation(out=tm["eT"][:nt, :nt], in_=tm["t1"][:nt, :nt], func=AF.Exp),
                         reads=[k("t1")], writes=[k("eT")])
                    S.op("act", lambda e: e.activation(out=tm["e"][:nt, :nt], in_=tm["t2"][:nt, :nt], func=AF.Exp, scale=-1.0),
                         reads=[k("t2")], writes=[k("e")])
                    S.op("act", lambda e: e.activation(out=sm[:nt, 1:2], in_=gcol, func=AF.Exp, scale=-1.0, bias=sm[:nt, 0:1]),
                         reads=[kx("sm0")], writes=[kx("sm1")])
                    S.op("act", lambda e: e.activation(out=sm[:, 2:3], in_=sm[:, 0:1], func=AF.Exp), reads=[kx("sm0")], writes=[kx("sm2")])
                    S.op("dve", lambda e: e.tensor_tensor(out=tm["decT"][:nt, :nt], in0=tm["eT"][:nt, :nt], in1=Umask[:nt, :nt],
                                                          op=ALU.mult), reads=[k("eT")], writes=[k("decT")])
                    S.op("dve", lambda e: e.tensor_tensor(out=tm["dec"][:nt, :nt], in0=tm["e"][:nt, :nt], in1=SLmask[:nt, :nt],
                                                          op=ALU.mult), reads=[k("e")], writes=[k("dec")])
                    yield
                    S.op("dve", lambda e: e.tensor_scalar(out=xc["kd"][:nt, :], in0=tm["ktm"][:nt, :], scalar1=sm[:nt, 1:2], scalar2=None,
                                                          op0=ALU.mult), reads=[k("ktm"), kx("sm1")], writes=[kx("kd")])
                    S.op("dve", lambda e: e.scalar_tensor_tensor(out=tm["X"][:nt, :nt], in0=pK[:nt, 0:nt], scalar=nball[:nt, bi, h:h + 1],
                                                                 in1=tm["dec"][:nt, :nt], op0=ALU.mult, op1=ALU.mult),
                         reads=[pKk, k("dec")], writes=[k("X")])
                    S.op("dve", lambda e: e.tensor_tensor(out=xc["qkT"][:nt, :nt], in0=pK[:nt, 128:128 + nt], in1=tm["decT"][:nt, :nt],
                                                          op=ALU.mult), reads=[pKk, k("decT")], writes=[kx("qkT")])
                    yield
                    pX, pXk = nps()
                    NEUMI = int((build.phases or {}).get("neum", 2))
                    NEUBF = NEUMI != 0
                    pXv = pX[:].bitcast(BF16) if NEUMI == 1 else pX
                    pXo = {0: pX, 1: pXv, 2: pX[:].bitcast(F32R)}[NEUMI]
                    S.transposes([(pXo[:nt, 0:nt], tm["X"][:nt, :nt])], {0: ident, 1: identbf, 2: identr[:, :]}[NEUMI],
                                 reads=[k("X"), "identbf", "identr"], writes=[pXk])
                    S.op("act", lambda e: e.copy(out=tm["XT"][:nt, :nt], in_=pXv[:nt, 0:nt]), reads=[pXk], writes=[k("XT")])
                    S.op("dve", lambda e: e.tensor_tensor(out=tm["PT"][:nt, :nt], in0=pXv[:nt, 0:nt], in1=ident[:nt, :nt], op=ALU.add),
                         reads=[pXk], writes=[k("PT")])
                    PTop = "PT"
                    if NEUBF:
                        PTop = "PTm"
                        S.op("act", lambda e: e.copy(out=tm["PTm"][:nt, :nt], in_=tm["PT"][:nt, :nt]), reads=[k("PT")], writes=[k("PTm")])
                    yield
                    NR = 6 if nt == 128 else 3
                    Xc, XTc, Xn, XTn = "X", "XT", "X2", "XT2"
                    for r in range(1, NR + 2):
                        pA, pAk = nps()
                        grp = []
                        if r <= NR:
                            grp.append((pA[:nt, 0:nt], [(tm[XTc][:nt, :nt], tm[Xc][:nt, :nt])]))
                            if r < NR:
                                grp.append((pA[:nt, 128:128 + nt], [(tm[Xc][:nt, :nt], tm[XTc][:nt, :nt])]))
                        if r >= 2:
                            grp.append((pA[:nt, 256:256 + nt], [(tm[Xc][:nt, :nt], tm[PTop][:nt, :nt])]))
                        S.mmg(grp, reads=[k(Xc), k(XTc), k(PTop)], writes=[pAk])
                        if r <= NR:
                            S.op("act", lambda e: e.copy(out=tm[Xn][:nt, :nt], in_=pA[:nt, 0:nt]), reads=[pAk], writes=[k(Xn)])
                            if r < NR:
                                S.op("act", lambda e: e.copy(out=tm[XTn][:nt, :nt], in_=pA[:nt, 128:128 + nt]), reads=[pAk], writes=[k(XTn)])
                        if r >= 2:
                            S.op("dve", lambda e: e.tensor_tensor(out=tm["PT"][:nt, :nt], in0=tm["PT"][:nt, :nt], in1=pA[:nt, 256:256 + nt],
                                                                  op=ALU.add), reads=[pAk, k("PT")], writes=[k("PT")])
                            if NEUBF and r <= NR:
                                S.op("act", lambda e: e.copy(out=tm["PTm"][:nt, :nt], in_=tm["PT"][:nt, :nt]), reads=[k("PT")], writes=[k("PTm")])
                        Xc, XTc, Xn, XTn = Xn, XTn, Xc, XTc
                        yield
                    S.op("act", lambda e: e.copy(out=tm["PTbf"][:nt, :nt], in_=tm["PT"][:nt, :nt]), reads=[k("PT")], writes=[k("PTbf")])
                    yield
                    pU, pUk = nps()
                    S.mmg([(pU[:nt, 0:128], [(tm["PTbf"][:nt, :nt], tm["vb"][:nt, :])]),
                           (pU[:, 128:128 + nt], [(tm["kbg"][:nt, :], tm["PTbf"][:nt, :nt])])],
                          reads=[k("PTbf"), k("vb"), k("kbg")], writes=[pUk])
                    S.op("act", lambda e: e.copy(out=xc["u"][:nt, :], in_=pU[:nt, 0:128]), reads=[pUk], writes=[kx("u")])
                    S.op("dve", lambda e: e.tensor_copy(out=xc["wT"][:, :nt], in_=pU[:, 128:128 + nt]), reads=[pUk], writes=[kx("wT")])
                    yield

                def gen_scan(h, hs, bi, t0, nt, zg, zgk):
                    j = hs["j"]
                    tm = G["TMP"][j]
                    xc = G["XCH"][(j, bi % 2)]
                    k = lambda n: ("g", n, j)
                    kx = lambda n: ("x", n, j, bi % 2)
                    pS, pSk = ps[6 + (j % 2)], ("ps", 6 + (j % 2))
                    blk = slice(t0, t0 + nt)
                    hk = [("qk", j, 0), ("qk", j, 1), ("vsT", j)]
                    sm = xc["sm"]
                    if bi == 0:
                        S.op("dve", lambda e: e.memset(hs["S"][:], 0.0), writes=[k("S")])
                        S.op("dve", lambda e: e.memset(hs["Sbf"][:], 0.0), writes=[k("Sbf")])
                    elif bi >= 16:
                        S.dma("sp", hs["S"][:], gin("state_gdn")[bi - 16, h, :, :], writes=[k("S")])
                        S.op("act", lambda e: e.copy(out=hs["Sbf"][:], in_=hs["S"][:]), reads=[k("S")], writes=[k("Sbf")])
                    S.mmg([(pS[:nt, 0:128], [(xc["wT"][:, :nt], hs["Sbf"][:, :])]),
                           (pS[:nt, 128:256], [(hs["qhT"][:, blk], hs["Sbf"][:, :])])], reads=[kx("wT"), k("Sbf")] + hk, writes=[pSk])
                    S.op("dve", lambda e: e.tensor_tensor(out=tm["vnew"][:nt, :], in0=xc["u"][:nt, :], in1=pS[:nt, 0:128], op=ALU.subtract),
                         reads=[pSk, kx("u")], writes=[k("vnew")])
                    S.op("act", lambda e: e.activation(out=tm["oA"][:nt, :], in_=pS[:nt, 128:256], func=AF.Copy, scale=egc[:nt, bi, h:h + 1]),
                         reads=[pSk], writes=[k("oA")])
                    yield
                    S.mmg([(pS[:nt, 256:384], [(xc["qkT"][:nt, :nt], tm["vnew"][:nt, :])]),
                           (pS[:, 384:512], [(xc["kd"][:nt, :], tm["vnew"][:nt, :])])],
                          reads=[kx("qkT"), k("vnew"), kx("kd")], writes=[pSk])
                    S.op("dve", lambda e: e.scalar_tensor_tensor(out=hs["S"][:, :], in0=hs["S"][:, :], scalar=sm[:, 2:3], in1=pS[:, 384:512],
                                                                 op0=ALU.mult, op1=ALU.add), reads=[pSk, k("S"), kx("sm2")], writes=[k("S")])
                    S.op("dve", lambda e: e.tensor_tensor(out=tm["o"][:nt, :], in0=tm["oA"][:nt, :], in1=pS[:nt, 256:384], op=ALU.add),
                         reads=[pSk, k("oA")], writes=[k("o")])
                    S.op("act", lambda e: e.copy(out=hs["Sbf"][:, :], in_=hs["S"][:, :]), reads=[k("S")], writes=[k("Sbf")])
                    if bi >= 15:
                        S.dma("sp", gs_o[max(0, bi - 15), h, :, :], hs["S"][:, :], reads=[k("S")])
                    yield
                    S.op("act", lambda e: e.activation(out=tm["jk"][:nt, :], in_=tm["o"][:nt, :], func=AF.Square, accum_out=sm[:nt, 3:4]),
                         reads=[k("o")], writes=[k("jk"), kx("sm3")])
                    S.op("act", lambda e: e.activation(out=sm[:nt, 4:5], in_=sm[:nt, 3:4], func=AF.Sqrt, scale=1.0 / 128, bias=1e-6),
                         reads=[kx("sm3")], writes=[kx("sm4")])
                    yield
                    S.op("dve", lambda e: e.reciprocal(out=sm[:nt, 5:6], in_=sm[:nt, 4:5]), reads=[kx("sm4")], writes=[kx("sm5")])
                    S.op("dve", lambda e: e.scalar_tensor_tensor(out=tm["ob"][:nt, :], in0=tm["o"][:nt, :], scalar=sm[:nt, 5:6],
                                                                 in1=zg[:nt, j * 128:(j + 1) * 128], op0=ALU.mult, op1=ALU.mult),
                         reads=[k("o"), kx("sm5"), zgk], writes=[k("ob")])
                    yield
                    tqb = pS[:].bitcast(BF16)
                    S.transposes([(tqb[:, 0:nt], tm["ob"][:nt, :])], identbf, reads=[k("ob"), "identbf"], writes=[pSk])
                    S.op("act", lambda e: e.copy(out=hs["obT"][:, blk], in_=tqb[:, 0:nt]), reads=[pSk], writes=[("obT", j)])
                    yield

                def lockstep(gens):
                    live = list(gens)
                    while live:
                        nxt = []
                        for g in live:
                            try:
                                next(g)
                                nxt.append(g)
                            except StopIteration:
                                pass
                        live = nxt

                for h0 in range(0, NGH, HG):
                    hl = list(range(h0, min(NGH, h0 + HG)))
                    with contextlib.ExitStack() as sa:
                        alloc_proj(sa)
                        for h in hl:
                            gdn_proj(h, heads[h - h0])
                        S.barrier()
                    with contextlib.ExitStack() as sb:
                        alloc_blk(sb)
                        nh_ = len(hl)
                        S.dma("pool", G["WZ2"][:, :, 0:nh_ * 128], w_in_v()[:, :, GZ + h0 * 128:GZ + (h0 + nh_) * 128], writes=["wz2"])
                        for step in range(NB + 1):
                            gens = []
                            if step < NB:
                                t0, nt = BLOCKS[step]
                                gens += [gen_solve(h, heads[h - h0], step, t0, nt) for h in hl]
                            if step >= 1:
                                t0, nt = BLOCKS[step - 1]
                                zg, zgk = G["ZG"].next()
                                gens += [gen_gz(step - 1, t0, nt, zg, zgk, nh_)]
                                gens += [gen_scan(h, heads[h - h0], step - 1, t0, nt, zg, zgk) for h in hl]
                            lockstep(gens)
                        for h in hl:
                            S.dma("sp", obT_d[h * 128:(h + 1) * 128, :], heads[h - h0]["obT"][:], reads=[("obT", h - h0)])
                        S.barrier()
                S.barrier()


        if PH["post"]:
            with contextlib.ExitStack() as st:
                WGT = Rot([sbuf(st, "wgt%d" % i, [128, 16, 512], BF16) for i in range(2)], "wgt")
                SG = Rot([sbuf(st, "sg%d" % i, [128, T], BF16) for i in range(3)], "sg")
                pg = psrot([0, 1, 2, 3, 4, 5, 6, 7], "ps")
                for c4 in range(8):
                    wt, wtk = WGT.next()
                    S.dma("pool", wt[:], w_in_v()[:, :, GTA + c4 * 512:GTA + (c4 + 1) * 512], writes=[wtk])
                    for cc in range(4):
                        c = c4 * 4 + cc
                        sg, sgk = SG.next()
                        for (s0, sn) in SPANS:
                            p, pk = pg.next()
                            S.mm(p[:, :sn], [(wt[:, kc, cc * 128:(cc + 1) * 128], xT[:, kc, s0:s0 + sn]) for kc in range(16)],
                                 reads=[wtk], writes=[pk])
                            S.op("act", lambda e: e.activation(out=sg[:, s0:s0 + sn], in_=p[:, :sn], func=AF.Sigmoid),
                                 reads=[pk], writes=[sgk])
                        S.dma("sp", sgT_d[c * 128:(c + 1) * 128, :], sg[:], reads=[sgk], writes=["sgT_d"])
                S.barrier()
        xscope.close()

        if PH["post"]:
            lnp = gin("lnp")
            HALF = [(list(range(0, 8)), 0, 1024), (list(range(8, NB)), 1024, T - 1024)]
            for hf, (hblks, h0, hn) in enumerate(HALF):
                hspans = [(i, min(512, hn - i)) for i in range(0, hn, 512)]
                with contextlib.ExitStack() as sth:
                    bufA = sbuf(sth, "bufA", [128, 16, 1056], BF16)
                    lsm = sbuf(sth, "lsm", [128, 32])
                    L = {}

                    def alloc_ln(sc, full=True):
                        L["lng"] = sbuf(sc, "lng", [128, D])
                        L["lnb"] = sbuf(sc, "lnb", [128, D])
                        L["hp"] = Rot([sbuf(sc, "hp%d" % i, [128, D]) for i in range(2 if full else 1)], "hp")
                        if full:
                            L["xb"] = Rot([sbuf(sc, "xb%d" % i, [128, D]) for i in range(2)], "xb")
                            L["hbf"] = Rot([sbuf(sc, "hbf%d" % i, [128, D], BF16) for i in range(2)], "hbf")
                    pg = psrot([0, 1, 2, 3, 4, 5], "ps")
                    pT2 = psrot([6, 7], "ps")
                    evc = [0]

                    def load_ln(i):
                        S.dma("sp", L["lng"][:], lnp[2 * i:2 * i + 1, :].partition_broadcast(128), writes=["lng"])
                        S.dma("sp", L["lnb"][:], lnp[2 * i + 1:2 * i + 2, :].partition_broadcast(128), writes=["lnb"])

                    def ln_block(h, hk, nt, eps, dstT, lt0, out_d, t0):
                        st6 = lsm[:, 0:24].rearrange("p (c s) -> p c s", c=4)
                        for c in range(4):
                            S.op("dve", lambda e: e.bn_stats(out=st6[:nt, c, :], in_=h[:nt, c * 512:(c + 1) * 512]),
                                 reads=[hk], writes=[("lsm", c)])
                        S.op("dve", lambda e: e.bn_aggr(out=lsm[:nt, 24:26], in_=st6[:nt, :, :]),
                             reads=[("lsm", c) for c in range(4)], writes=["lmv"])
                        S.op("act", lambda e: e.activation(out=lsm[:nt, 26:27], in_=lsm[:nt, 25:26], func=AF.Sqrt, bias=eps),
                             reads=["lmv"], writes=["lrs"])
                        S.op("dve", lambda e: e.reciprocal(out=lsm[:nt, 27:28], in_=lsm[:nt, 26:27]), reads=["lrs"], writes=["lrr"])
                        S.op("dve", lambda e: e.tensor_scalar(out=h[:nt, :], in0=h[:nt, :], scalar1=lsm[:nt, 24:25], scalar2=lsm[:nt, 27:28],
                                                              op0=ALU.subtract, op1=ALU.mult), reads=[hk, "lmv", "lrr"], writes=[hk])
                        S.op("dve", lambda e: e.tensor_tensor(out=h[:nt, :], in0=h[:nt, :], in1=L["lng"][:nt, :], op=ALU.mult),
                             reads=[hk, "lng"], writes=[hk])
                        S.op("dve", lambda e: e.tensor_tensor(out=h[:nt, :], in0=h[:nt, :], in1=L["lnb"][:nt, :], op=ALU.add),
                             reads=[hk, "lnb"], writes=[hk])
                        S.dma("sp", out_d[t0:t0 + nt, :], h[:nt, :], reads=[hk], writes=[("od", t0)])
                        if dstT is None:
                            return None
                        hb, hbk = L["hbf"].next()
                        S.op("act", lambda e: e.copy(out=hb[:nt, :], in_=h[:nt, :]), reads=[hk], writes=[hbk])

                        def tail():
                            for g in range(4):
                                tp, tpk = pT2.next()
                                tpb = tp[:].bitcast(BF16)
                                S.transposes([(tpb[:, jj * 128:jj * 128 + nt], hb[:nt, (4 * g + jj) * 128:(4 * g + jj + 1) * 128])
                                              for jj in range(4)], identbf, reads=[hbk, "identbf"], writes=[tpk])
                                src = tpb[:, 0:512].rearrange("p (a b) -> p a b", a=4)[:, :, :nt]
                                dd = dstT[:, 4 * g:4 * g + 4, lt0:lt0 + nt]
                                evc[0] += 1
                                if evc[0] % 2:
                                    S.op("act", lambda e: e.copy(out=dd, in_=src), reads=[tpk], writes=[("dT", lt0)])
                                else:
                                    S.op("dve", lambda e: e.tensor_copy(out=dd, in_=src), reads=[tpk], writes=[("dT", lt0)])
                        return tail

                    with contextlib.ExitStack() as st:
                        bufB = sbuf(st, "bufB", [128, 16, 1056], BF16)
                        with contextlib.ExitStack() as s1:
                            inA = sbuf(s1, "inA", [128, 16, 1056], BF16)
                            WPA = Rot([sbuf(s1, "wpa%d" % i, [128, 16, 512], BF16) for i in range(2)], "wpa")
                            SGA = Rot([sbuf(s1, "sga%d" % i, [128, 1056], BF16) for i in range(2)], "sga")
                            m1 = Rot([sbuf(s1, "m1_%d" % i, [128, 512]) for i in range(2)], "m1")
                            for src_i, (src_d, w_d) in enumerate(((oaT_d, "w_pa"), (obT_d, "w_pb"))):
                                S.dma("sp", inA[:, :, 0:hn], src_d[:, h0:h0 + hn].rearrange("(kc p) t -> p kc t", p=128),
                                      reads=[], writes=["inA"])
                                for c4 in range(4):
                                    wt, wtk = WPA.next()
                                    S.dma("pool", wt[:], gin(w_d).rearrange("(kc p) n -> p kc n", p=128)[:, :, c4 * 512:(c4 + 1) * 512],
                                          writes=[wtk])
                                    for cc in range(4):
                                        c = c4 * 4 + cc
                                        sg, sgk = SGA.next()
                                        S.dma("sp", sg[:, 0:hn], sgT_d[(src_i * 16 + c) * 128:(src_i * 16 + c + 1) * 128, h0:h0 + hn],
                                              writes=[sgk])
                                        for (s0, sn) in hspans:
                                            p, pk = pg.next()
                                            S.mm(p[:, :sn], [(wt[:, kc, cc * 128:(cc + 1) * 128], inA[:, kc, s0:s0 + sn]) for kc in range(16)],
                                                 reads=[wtk, "inA"], writes=[pk])
                                            if src_i == 0:
                                                S.op("dve", lambda e: e.tensor_tensor(out=bufA[:, c, s0:s0 + sn], in0=p[:, :sn],
                                                                                      in1=sg[:, s0:s0 + sn], op=ALU.mult),
                                                     reads=[pk, sgk], writes=[("mix", c)])
                                            else:
                                                mm_, mk_ = m1.next()
                                                S.op("dve", lambda e: e.tensor_tensor(out=mm_[:, :sn], in0=p[:, :sn], in1=sg[:, s0:s0 + sn],
                                                                                      op=ALU.mult), reads=[pk, sgk], writes=[mk_])
                                                S.op("dve", lambda e: e.tensor_tensor(out=bufA[:, c, s0:s0 + sn], in0=bufA[:, c, s0:s0 + sn],
                                                                                      in1=mm_[:, :sn], op=ALU.add),
                                                     reads=[mk_, ("mix", c)], writes=[("mix", c)])
                            S.barrier()
                        with contextlib.ExitStack() as s2:
                            alloc_ln(s2)
                            wo = sbuf(s2, "wo", [128, 16, D], BF16)
                            S.dma("pool", wo[:], gin("w_o").rearrange("(kc p) n -> p kc n", p=128), writes=["wo"])
                            load_ln(0)
                            pend_tail = None
                            for bi in hblks:
                                t0, nt = BLOCKS[bi]
                                lt0 = t0 - h0
                                x_, xk = L["xb"].next()
                                S.dma("sp", x_[:nt, :], gin("x_all")[t0:t0 + nt, :], writes=[xk])
                                h_, hk = L["hp"].next()
                                for ct in range(4):
                                    p, pk = pg.next()
                                    S.mm(p[:nt, :], [(bufA[:, kc, lt0:lt0 + nt], wo[:, kc, ct * 512:(ct + 1) * 512]) for kc in range(16)],
                                         reads=["wo"], writes=[pk])
                                    S.op("dve", lambda e: e.scalar_tensor_tensor(out=h_[:nt, ct * 512:(ct + 1) * 512],
                                                                                 in0=x_[:nt, ct * 512:(ct + 1) * 512], scalar=ALPHA,
                                                                                 in1=p[:nt, :], op0=ALU.mult, op1=ALU.add),
                                         reads=[pk, xk], writes=[hk])
                                if pend_tail:
                                    pend_tail()
                                pend_tail = ln_block(h_, hk, nt, 1e-5, bufB, lt0, h1_d, t0)
                            if pend_tail:
                                pend_tail()
                            S.barrier()
                        with contextlib.ExitStack() as s3:
                            alloc_ln(s3)
                            wxq = sbuf(s3, "wxq", [128, 16, 512], BF16)
                            wxo = sbuf(s3, "wxo", [128, 4, D], BF16)
                            qxT = sbuf(s3, "qxT", [128, 4, 1056], BF16)
                            oxT = sbuf(s3, "oxT", [128, 4, 1056], BF16)
                            EX = Rot([sbuf(s3, "ex%d" % i, [128, 2, 128], BF16) for i in range(3)], "ex")
                            oxb = Rot([sbuf(s3, "oxb%d" % i, [128, 512], BF16) for i in range(2)], "oxb")
                            xsm = Rot([sbuf(s3, "xsm%d" % i, [128, 4]) for i in range(3)], "xsm")
                            S.dma("pool", wxq[:], gin("w_xq").rearrange("(kc p) n -> p kc n", p=128), writes=["wxq"])
                            S.dma("pool", wxo[:], gin("w_xo").rearrange("(kc p) n -> p kc n", p=128), writes=["wxo"])
                            for hd in range(4):
                                for (s0, sn) in hspans:
                                    p, pk = pg.next()
                                    S.mm(p[:, :sn], [(wxq[:, kc, hd * 128:(hd + 1) * 128], bufB[:, kc, s0:s0 + sn]) for kc in range(16)],
                                         reads=["wxq"], writes=[pk])
                                    S.op("act", lambda e: e.activation(out=qxT[:, hd, s0:s0 + sn], in_=p[:, :sn], func=AF.Copy,
                                                                       scale=128.0 ** -0.5), reads=[pk], writes=["qxT"])
                            for bi in hblks:
                                t0, nt = BLOCKS[bi]
                                lt0 = t0 - h0
                                mi = 0 if bi < 16 else 1 + (bi - 16)
                                ob_, obk = oxb.next()
                                for hd in range(4):
                                    stp, stk = pg.next()
                                    S.mmg([(stp[:, mc * 128:mc * 128 + nt], [(memKT[:, mi, hd, mc * 128:(mc + 1) * 128], qxT[:, hd, lt0:lt0 + nt])])
                                           for mc in range(2)], reads=["qxT"], writes=[stk])
                                    ex, exk = EX.next()
                                    S.op("act", lambda e: e.activation(out=ex[:, :, :nt],
                                                                       in_=stp[:, 0:256].rearrange("p (m q) -> p m q", m=2)[:, :, :nt],
                                                                       func=AF.Exp), reads=[stk], writes=[exk])
                                    po, pok = pg.next()
                                    S.mm(po[:nt, 0:129], [(ex[:, mc, :nt], memV[:, (mi * 2 + mc) * 4 + hd, 0:129]) for mc in range(2)],
                                         reads=[exk], writes=[pok])
                                    xs_, xsk = xsm.next()
                                    S.op("dve", lambda e: e.reciprocal(out=xs_[:nt, 0:1], in_=po[:nt, 128:129]), reads=[pok], writes=[xsk])
                                    S.op("act", lambda e: e.activation(out=ob_[:nt, hd * 128:(hd + 1) * 128], in_=po[:nt, 0:128], func=AF.Copy,
                                                                       scale=xs_[:nt, 0:1]), reads=[pok, xsk], writes=[obk])
                                tp, tpk = pT2.next()
                                tpb = tp[:].bitcast(BF16)
                                S.transposes([(tpb[:, hd * 128:hd * 128 + nt], ob_[:nt, hd * 128:(hd + 1) * 128]) for hd in range(4)],
                                             identbf, reads=[obk, "identbf"], writes=[tpk])
                                S.op("dve", lambda e: e.tensor_copy(out=oxT[:, :, lt0:lt0 + nt],
                                                                    in_=tpb[:, 0:512].rearrange("p (a b) -> p a b", a=4)[:, :, :nt]),
                                     reads=[tpk], writes=["oxT"])
                            load_ln(1)
                            pend_tail = None
                            for bi in hblks:
                                t0, nt = BLOCKS[bi]
                                lt0 = t0 - h0
                                x_, xk = L["xb"].next()
                                S.dma("sp", x_[:nt, :], h1_d[t0:t0 + nt, :], reads=[("od", t0)], writes=[xk])
                                h_, hk = L["hp"].next()
                                for ct in range(4):
                                    p, pk = pg.next()
                                    S.mm(p[:nt, :], [(oxT[:, kc, lt0:lt0 + nt], wxo[:, kc, ct * 512:(ct + 1) * 512]) for kc in range(4)],
                                         reads=["wxo", "oxT"], writes=[pk])
                                    S.op("dve", lambda e: e.scalar_tensor_tensor(out=h_[:nt, ct * 512:(ct + 1) * 512],
                                                                                 in0=x_[:nt, ct * 512:(ct + 1) * 512], scalar=ALPHA,
                                                                                 in1=p[:nt, :], op0=ALU.mult, op1=ALU.add),
                                         reads=[pk, xk], writes=[hk])
                                if pend_tail:
                                    pend_tail()
                                pend_tail = ln_block(h_, hk, nt, 1e-5, bufA, lt0, h2_d, t0)
                            if pend_tail:
                                pend_tail()
                            S.barrier()
                    with contextlib.ExitStack() as s5:
                        fT = sbuf(s5, "fT", [128, 44, 1056], BF16)
                        with contextlib.ExitStack() as s5a:
                            W1 = Rot([sbuf(s5a, "w1_%d" % i, [128, 16, 256], BF16) for i in range(2)], "w1")
                            W3 = Rot([sbuf(s5a, "w3_%d" % i, [128, 16, 256], BF16) for i in range(2)], "w3")
                            s1t = Rot([sbuf(s5a, "s1t%d" % i, [128, 512]) for i in range(3)], "s1t")
                            w1v = gin("w_ff1").rearrange("(kc p) n -> p kc n", p=128)
                            w3v = gin("w_ff3").rearrange("(kc p) n -> p kc n", p=128)
                            for f2 in range(22):
                                w1, w1k = W1.next()
                                w3, w3k = W3.next()
                                S.dma("pool", w1[:], w1v[:, :, f2 * 256:(f2 + 1) * 256], writes=[w1k])
                                S.dma("pool", w3[:], w3v[:, :, f2 * 256:(f2 + 1) * 256], writes=[w3k])
                                for cc in range(2):
                                    fc = f2 * 2 + cc
                                    for (s0, sn) in hspans:
                                        p1, p1k = pg.next()
                                        S.mm(p1[:, :sn], [(w1[:, kc, cc * 128:(cc + 1) * 128], bufA[:, kc, s0:s0 + sn]) for kc in range(16)],
                                             reads=[w1k], writes=[p1k])
                                        p3, p3k = pg.next()
                                        S.mm(p3[:, :sn], [(w3[:, kc, cc * 128:(cc + 1) * 128], bufA[:, kc, s0:s0 + sn]) for kc in range(16)],
                                             reads=[w3k], writes=[p3k])
                                        s1_, s1k = s1t.next()
                                        S.op("act", lambda e: e.activation(out=s1_[:, :sn], in_=p1[:, :sn], func=AF.Silu),
                                             reads=[p1k], writes=[s1k])
                                        S.op("dve", lambda e: e.tensor_tensor(out=fT[:, fc, s0:s0 + sn], in0=s1_[:, :sn], in1=p3[:, :sn],
                                                                              op=ALU.mult), reads=[s1k, p3k], writes=[("fT", fc)])
                            S.barrier()
                        with contextlib.ExitStack() as s5b:
                            W2 = Rot([sbuf(s5b, "w2_%d" % i, [128, 44, 512], BF16) for i in range(1)], "w2")
                            ypt = Rot([sbuf(s5b, "ypt%d" % i, [128, 512]) for i in range(2)], "ypt")
                            h2t = Rot([sbuf(s5b, "h2t%d" % i, [128, 512]) for i in range(2)], "h2t")
                            w2v = gin("w_ff2").rearrange("(fc p) n -> p fc n", p=128)
                            for ct in range(4):
                                w2, w2k = W2.next()
                                S.dma("pool", w2[:, 0:22, :], w2v[:, 0:22, ct * 512:(ct + 1) * 512], writes=[w2k])
                                S.dma("act", w2[:, 22:44, :], w2v[:, 22:44, ct * 512:(ct + 1) * 512], writes=[w2k]) if False else \
                                    S.dma("pool", w2[:, 22:44, :], w2v[:, 22:44, ct * 512:(ct + 1) * 512], writes=[w2k])
                                for bi in hblks:
                                    t0, nt = BLOCKS[bi]
                                    lt0 = t0 - h0
                                    p, pk = pg.next()
                                    S.mm(p[:nt, :], [(fT[:, fc, lt0:lt0 + nt], w2[:, fc, :]) for fc in range(44)], reads=[w2k], writes=[pk])
                                    hh, hhk = h2t.next()
                                    S.dma("sp", hh[:nt, :], h2_d[t0:t0 + nt, ct * 512:(ct + 1) * 512], reads=[("od", t0)], writes=[hhk])
                                    yy, yyk = ypt.next()
                                    S.op("dve", lambda e: e.scalar_tensor_tensor(out=yy[:nt, :], in0=hh[:nt, :], scalar=ALPHA, in1=p[:nt, :],
                                                                                 op0=ALU.mult, op1=ALU.add), reads=[pk, hhk], writes=[yyk])
                                    S.dma("sp", yp_d[t0:t0 + nt, ct * 512:(ct + 1) * 512], yy[:nt, :], reads=[yyk], writes=[("yp", t0)])
                            S.barrier()
                        with contextlib.ExitStack() as s5c:
                            alloc_ln(s5c, False)
                            load_ln(2)
                            for bi in hblks:
                                t0, nt = BLOCKS[bi]
                                h_, hk = L["hp"].next()
                                S.dma("sp", h_[:nt, :], yp_d[t0:t0 + nt, :], reads=[("yp", t0)], writes=[hk])
                                ln_block(h_, hk, nt, 1e-5, None, 0, y_o, t0)
                            S.barrier()

        S.finish()
    return nc


build.phases = None


def _consts():
    c = np.zeros((128, NCONST), np.float32)
    p = np.arange(128)[:, None]
    f = np.arange(128)[None, :]
    c[:, C_ID:C_ID + 128] = (p == f)
    c[:, C_ONE:C_ONE + 128] = 1.0
    c[:, C_U:C_U + 128] = (f >= p)
    c[:, C_L:C_L + 128] = (p >= f)
    c[:, C_SL:C_SL + 128] = (p > f)
    slopes = 2.0 ** (-8.0 * np.arange(1, 9) / 8.0)
    for h in range(8):
        bd = -slopes[h] * np.abs(f - p) + slopes[h] * f - 64.0 * slopes[h]
        allowed = (p // 64) <= (f // 64)
        c[:, C_BD + h * 128:C_BD + (h + 1) * 128] = np.where(allowed, bd, -30000.0)
        for d in range(17):
            c[:, C_AB + h * 17 + d] = slopes[h] * (np.arange(128) - 128.0 * d - 64.0)
    return c


_NC_CACHE = {}


def kernel(**inp):
    import ml_dtypes
    dbg = bool(build.phases and build.phases.get("dbg"))
    key = (dbg, str(build.phases))
    if key not in _NC_CACHE:
        _NC_CACHE[key] = build(dbg)
    nc = _NC_CACHE[key]
    f = lambda a: np.ascontiguousarray(a, dtype=np.float32)
    consts = _consts()
    identbf = np.eye(128, dtype=np.float32).astype(ml_dtypes.bfloat16)
    shared = {
        "w_in": f(inp["w_in"][0]), "conv_w": f(inp["conv_w"][0]),
        "lamv": f(np.concatenate([inp["lam_q1"], inp["lam_k1"], inp["lam_q2"], inp["lam_k2"]], 1)),
        "diff_subln_g": f(inp["diff_subln_g"]), "gdn_a_log": f(inp["gdn_a_log"]), "gdn_dt_bias": f(inp["gdn_dt_bias"]),
        "gdn_norm_g": f(inp["gdn_norm_g"]), "w_pa": f(inp["w_pa"][0]), "w_pb": f(inp["w_pb"][0]), "w_o": f(inp["w_o"][0]),
        "lnp": f(np.concatenate([inp["ln1_g"], inp["ln1_b"], inp["ln2_g"], inp["ln2_b"], inp["ln3_g"], inp["ln3_b"]], 0)),
        "w_xq": f(inp["w_xq"][0]), "w_xk": f(inp["w_xk"][0]), "w_xv": f(inp["w_xv"][0]), "w_xo": f(inp["w_xo"][0]),
        "w_ff1": f(inp["w_ff1"][0]), "w_ff3": f(inp["w_ff3"][0]), "w_ff2": f(inp["w_ff2"][0]),
        "consts": consts, "ident_bf": identbf,
    }
    used = set(build.used_in.keys())
    shared = {k: v for k, v in shared.items() if k in used}
    in_maps = []
    for c in range(8):
        b = c % 4
        sl = slice(2 * c, 2 * c + 2)
        m = dict(shared)
        m["x_all"] = f(np.concatenate([inp["x_prompt"][b], inp["x_sample"][sl].reshape(NS * TS, D)], 0))
        m["mem"] = f(inp["mem_prompt"][b])
        m["cache_k"] = f(inp["cache_diff_k"][0, sl])
        m["cache_v"] = f(inp["cache_diff_v"][0, sl])
        m["state_gdn"] = f(inp["state_gdn"][0, sl])
        m["state_conv"] = f(inp["state_gdn_conv"][0, sl])
        m["cache_mem_k"] = f(inp["cache_mem_k"][0, sl])
        m["cache_mem_v"] = f(inp["cache_mem_v"][0, sl])
        in_maps.append({k: v for k, v in m.items() if k in used})
    res = run_bass_kernel_spmd(nc, in_maps, core_ids=list(range(8)))
    R = res.results
    kernel.last = R
    y_p = np.stack([R[b]["y"][:NPR] for b in range(4)])
    y_s = np.concatenate([R[c]["y"][NPR:].reshape(NS, TS, D) for c in range(8)], 0)
    dk_p = np.stack([R[b]["dk"][:NPR] for b in range(4)]).reshape(1, 4, NPR, 8, 256)
    dv_p = np.stack([R[b]["dv"][:NPR] for b in range(4)]).reshape(1, 4, NPR, 8, 256)
    gs_p = np.stack([R[b]["gstate"][0] for b in range(4)])[None]
    gc_p = np.stack([R[b]["gconv"][0] for b in range(4)])[None]
    mk_p = np.stack([R[b]["memk"] for b in range(4)]).reshape(1, 4, 256, 4, 128)
    mv_p = np.stack([R[b]["memv"] for b in range(4)]).reshape(1, 4, 256, 4, 128)
    dk_s = np.concatenate([R[c]["dk"][NPR:].reshape(NS, TS, 8, 256) for c in range(8)], 0)[None]
    dv_s = np.concatenate([R[c]["dv"][NPR:].reshape(NS, TS, 8, 256) for c in range(8)], 0)[None]
    gs_s = np.concatenate([R[c]["gstate"][1:] for c in range(8)], 0)[None]
    gc_s = np.concatenate([R[c]["gconv"][1:] for c in range(8)], 0)[None]
    outs = (y_p, y_s, dk_p, dv_p, gs_p, gc_p, mk_p, mv_p, dk_s, dv_s, gs_s, gc_s)
    return tuple(np.ascontiguousarray(o, dtype=np.float32) for o in outs)
```

```python
import contextlib
import numpy as np
import concourse.bass as bass
import concourse.mybir as mybir
from concourse.bass_utils import run_bass_kernel_spmd

F32 = mybir.dt.float32
BF16 = mybir.dt.bfloat16
AF = mybir.ActivationFunctionType
ALU = mybir.AluOpType

D = 2048
NPR = 2048
NS = 2
TS = 16
T = NPR + NS * TS
PAST = 1024
BLOCKS = [(i * 128, 128) for i in range(16)] + [(NPR + TS * s, TS) for s in range(NS)]
NB = len(BLOCKS)
SPANS = [(0, 512), (512, 512), (1024, 512), (1536, 512), (2048, NS * TS)]
DQ, DK, DV, GQ, GK, GV, GZ, GA, GB, GTA, GTB = 0, 2048, 4096, 6144, 8192, 10240, 12288, 14336, 14352, 14368, 16416
DFF = 5632
ALPHA = 2.0 ** 0.25
LAM_INIT = 0.2
C_ID, C_ONE, C_U, C_L, C_SL, C_BD, C_AB = 0, 128, 256, 384, 512, 640, 640 + 1024
NCONST = C_AB + 8 * 17


class Sch:
    NDMA = 12

    def __init__(self, nc, stack):
        self.nc = nc
        self.eng = dict(pe=nc.tensor, act=nc.scalar, dve=nc.vector, pool=nc.gpsimd, sp=nc.sync)
        self.sems = {}
        self.cnt = {}
        for e in ("pe", "act", "dve", "pool"):
            self.sems[e] = stack.enter_context(nc.semaphore("c_" + e))
            self.cnt[e] = 0
        self.dq = {}
        for q in ("sp", "pool", "act"):
            names = []
            for i in range(self.NDMA):
                n = "d_%s%d" % (q, i)
                self.sems[n] = stack.enter_context(nc.semaphore(n))
                self.cnt[n] = 0
                names.append(n)
            self.dq[q] = [names, 0]
        self.seen = {e: {} for e in self.eng}
        self.last_w = {}
        self.readers = {}
        self.n_ins = 0
        self.dead = False

    def _wait(self, e, toks):
        need = {}
        for t in toks:
            if t is None:
                continue
            if need.get(t[0], 0) < t[1]:
                need[t[0]] = t[1]
        seen = self.seen[e]
        for sem, val in need.items():
            if seen.get(sem, 0) >= val:
                continue
            self.eng[e].wait_ge(self.sems[sem], val)
            seen[sem] = val

    def _deps(self, reads, writes):
        toks = []
        for k in reads:
            toks.append(self.last_w.get(k))
        for k in writes:
            toks.append(self.last_w.get(k))
            r = self.readers.get(k)
            if r:
                toks.extend(r.items())
        return toks

    def _record(self, tok, reads, writes):
        for k in reads:
            r = self.readers.setdefault(k, {})
            if r.get(tok[0], 0) < tok[1]:
                r[tok[0]] = tok[1]
        for k in writes:
            self.last_w[k] = tok
            self.readers[k] = {}

    def op(self, e, fn, reads=(), writes=()):
        if self.dead:
            return None
        psr = [k for k in reads if isinstance(k, tuple) and k[0] == "ps" and k not in writes]
        if psr:
            writes = list(writes) + psr
        self._wait(e, self._deps(reads, writes))
        ins = fn(self.eng[e])
        self.cnt[e] += 1
        ins.then_inc(self.sems[e], 1)
        tok = (e, self.cnt[e])
        self._record(tok, reads, writes)
        self.n_ins += 1
        return tok

    def mmg(self, groups, reads=(), writes=()):
        if self.dead:
            return None
        self._wait("pe", self._deps(reads, writes))
        ins = None
        for out, pairs in groups:
            n = len(pairs)
            for i, (l, r) in enumerate(pairs):
                ins = self.nc.tensor.matmul(out, l, r, start=(i == 0), stop=(i == n - 1))
                self.n_ins += 1
        self.cnt["pe"] += 1
        ins.then_inc(self.sems["pe"], 1)
        tok = ("pe", self.cnt["pe"])
        self._record(tok, reads, writes)
        return tok

    def mm(self, out, pairs, reads=(), writes=()):
        return self.mmg([(out, pairs)], reads, writes)

    def mm1(self, out, l, r, start, stop, reads=(), writes=()):
        if self.dead:
            return None
        self._wait("pe", self._deps(reads, writes))
        ins = self.nc.tensor.matmul(out, l, r, start=start, stop=stop)
        self.cnt["pe"] += 1
        ins.then_inc(self.sems["pe"], 1)
        tok = ("pe", self.cnt["pe"])
        self._record(tok, reads, writes)
        self.n_ins += 1
        return tok

    def transposes(self, items, ident, reads=(), writes=()):
        if self.dead:
            return None
        self._wait("pe", self._deps(reads, writes))
        ins = None
        for out, in_ in items:
            k = in_.shape[0]
            ins = self.nc.tensor.transpose(out, in_, ident[:k, :k])
            self.n_ins += 1
        self.cnt["pe"] += 1
        ins.then_inc(self.sems["pe"], 1)
        tok = ("pe", self.cnt["pe"])
        self._record(tok, reads, writes)
        return tok

    def dma(self, q, out, in_, reads=(), writes=(), **kw):
        if self.dead:
            return None
        names, idx = self.dq[q]
        n = names[idx % self.NDMA]
        self.dq[q][1] = idx + 1
        prev = (n, self.cnt[n]) if self.cnt[n] else None
        self._wait(q, self._deps(reads, writes) + [prev])
        ins = self.eng[q].dma_start(out=out, in_=in_, **kw)
        self.cnt[n] += 16
        ins.then_inc(self.sems[n], 16)
        tok = (n, self.cnt[n])
        self._record(tok, reads, writes)
        self.n_ins += 1
        return tok

    def all_toks(self):
        toks = []
        for q, (names, _) in self.dq.items():
            for n in names:
                if self.cnt[n]:
                    toks.append((n, self.cnt[n]))
        for e in ("pe", "act", "dve", "pool"):
            if self.cnt[e]:
                toks.append((e, self.cnt[e]))
        return toks

    def barrier(self):
        if self.dead:
            return None
        toks = self.all_toks()
        for e in ("pe", "act", "dve", "pool", "sp"):
            self._wait(e, toks)
        self.last_w = {}
        self.readers = {}

    def finish(self):
        self._wait("sp", self.all_toks())


class Rot:
    def __init__(self, items, name):
        self.items = items
        self.name = name
        self.i = 0

    def next(self):
        j = self.i % len(self.items)
        self.i += 1
        return self.items[j], (self.name, j)


def build(dbg=False):
    nc = bass.Bass("TRN2", target_bir_lowering=False)

    def din(n, sh, dt=F32):
        return nc.dram_tensor(n, sh, dt, kind="ExternalInput").ap()

    def dout(n, sh, dt=F32):
        return nc.dram_tensor(n, sh, dt, kind="ExternalOutput").ap()

    def dscr(n, sh, dt=F32):
        return nc.dram_tensor(n, sh, dt, kind="Internal").ap()

    IN_SHAPES = {
        "x_all": [T, D], "mem": [256, D], "cache_k": [NS, PAST, 8, 256], "cache_v": [NS, PAST, 8, 256],
        "state_gdn": [NS, 16, 128, 128], "state_conv": [NS, 3, 6144], "cache_mem_k": [NS, 256, 4, 128],
        "cache_mem_v": [NS, 256, 4, 128], "w_in": [D, 18464], "conv_w": [4, 6144], "lamv": [1, 512],
        "diff_subln_g": [1, 256], "gdn_a_log": [1, 16], "gdn_dt_bias": [1, 16], "gdn_norm_g": [1, 128],
        "w_pa": [D, D], "w_pb": [D, D], "w_o": [D, D], "lnp": [6, D], "w_xq": [D, 512], "w_xk": [D, 512],
        "w_xv": [D, 512], "w_xo": [512, D], "w_ff1": [D, DFF], "w_ff3": [D, DFF], "w_ff2": [DFF, D],
        "consts": [128, NCONST], "ident_bf": [128, 128],
    }
    used_in = {}

    def gin(n):
        if n not in used_in:
            used_in[n] = din(n, IN_SHAPES[n], BF16 if n == "ident_bf" else F32)
        return used_in[n]

    build.used_in = used_in

    y_o = dout("y", [T, D])
    dk_o = dout("dk", [T, D])
    dv_o = dout("dv", [T, D])
    gs_o = dout("gstate", [1 + NS, 16, 128, 128])
    gc_o = dout("gconv", [1 + NS, 3, 6144])
    mk_o = dout("memk", [256, 512])
    mv_o = dout("memv", [256, 512])

    oaT_d = (dout if dbg else dscr)("oaT", [D, T], BF16)
    obT_d = (dout if dbg else dscr)("obT", [D, T], BF16)
    sgT_d = dscr("sgT", [2 * D, T], BF16)
    yp_d = dscr("yps", [T, D])
    h1_d = (dout if dbg else dscr)("h1s", [T, D])
    h2_d = (dout if dbg else dscr)("h2s", [T, D])

    def w_in_v():
        return gin("w_in").rearrange("(kc p) n -> p kc n", p=128)

    with contextlib.ExitStack() as top:
        S = Sch(nc, top)

        uniq = [0]

        def sbuf(st, n, sh, dt=F32):
            uniq[0] += 1
            return st.enter_context(nc.sbuf_tensor("%s_u%d" % (n, uniq[0]), sh, dt))

        ps = [top.enter_context(nc.psum_tensor("ps%d" % i, [128, 512], F32)) for i in range(8)]

        def psrot(idx, name):
            return Rot([ps[i] for i in idx], name)

        cst = sbuf(top, "cst", [128, NCONST])
        identbf = sbuf(top, "identbf", [128, 128], BF16)
        lamt = sbuf(top, "lamt", [128, 4])
        memKT = sbuf(top, "memKT", [128, 1 + NS, 4, 256], BF16)
        memV = sbuf(top, "memV", [128, (1 + NS) * 8, 130], BF16)
        xscope = contextlib.ExitStack()
        xT = sbuf(xscope, "xT", [128, 16, T], BF16)
        ident = cst[:, C_ID:C_ID + 128]
        ones = cst[:, C_ONE:C_ONE + 128]
        Umask = cst[:, C_U:C_U + 128]
        SLmask = cst[:, C_SL:C_SL + 128]

        stop = (build.phases or {}).get("stop", 99)

        class StopBuild(Exception):
            pass

        def gate(k):
            if stop <= k:
                S.dead = True

        try:
          with contextlib.ExitStack() as st:
              S.dma("sp", cst[:], gin("consts")[:, :], writes=["cst"])
              S.dma("sp", identbf[:], gin("ident_bf")[:, :], writes=["identbf"])
              lv = sbuf(st, "lv", [128, 4, 128])
              S.dma("sp", lv[:].rearrange("p a b -> p (a b)"), gin("lamv")[0:1, :].partition_broadcast(128), writes=["lv"])
              lp = sbuf(st, "lp", [128, 2, 128])
              S.op("dve", lambda e: e.tensor_tensor(out=lp[:], in0=lv[:, 0:4:2, :], in1=lv[:, 1:4:2, :], op=ALU.mult),
                   reads=["lv"], writes=["lp"])
              ls = sbuf(st, "ls", [128, 2])
              S.op("dve", lambda e: e.reduce_sum(out=ls[:], in_=lp[:], axis=mybir.AxisListType.X), reads=["lp"], writes=["ls"])
              le = sbuf(st, "le", [128, 2])
              S.op("act", lambda e: e.activation(out=le[:], in_=ls[:], func=AF.Exp), reads=["ls"], writes=["le"])
              S.op("dve", lambda e: e.tensor_tensor(out=lamt[:, 0:1], in0=le[:, 0:1], in1=le[:, 1:2], op=ALU.subtract),
                   reads=["le"], writes=["lamt"])
              S.op("dve", lambda e: e.tensor_scalar(out=lamt[:, 0:1], in0=lamt[:, 0:1], scalar1=LAM_INIT, scalar2=None,
                                                    op0=ALU.add), reads=["lamt"], writes=["lamt"])
              S.op("dve", lambda e: e.tensor_scalar(out=lamt[:, 1:2], in0=lamt[:, 0:1], scalar1=-1.0, scalar2=None,
                                                    op0=ALU.mult), reads=["lamt"], writes=["lamt"])
              gate(1)
              xs = Rot([sbuf(st, "xs%d" % i, [128, D]) for i in range(2)], "xs")
              pr = psrot([1, 2, 3, 4, 5, 6, 7], "ps")
              ei = 0

              def load_T(src_rows, nt, dst, t0):
                  nonlocal ei
                  b, bk = xs.next()
                  S.dma("sp", b[:nt, :], src_rows, writes=[bk])
                  for g in range(4):
                      p, pk = pr.next()
                      S.transposes([(p[:, j * 128:j * 128 + nt], b[:nt, (4 * g + j) * 128:(4 * g + j + 1) * 128])
                                    for j in range(4)], ident, reads=[bk, "cst"], writes=[pk])
                      src = p[:].rearrange("p (a b) -> p a b", a=4)[:, :, :nt]
                      dd = dst[:, 4 * g:4 * g + 4, t0:t0 + nt]
                      ei += 1
                      if ei % 2:
                          S.op("act", lambda e: e.copy(out=dd, in_=src), reads=[pk], writes=[("T", ei)])
                      else:
                          S.op("dve", lambda e: e.tensor_copy(out=dd, in_=src), reads=[pk], writes=[("T", ei)])

              for (t0, nt) in BLOCKS:
                  load_T(gin("x_all")[t0:t0 + nt, :], nt, xT, t0)
              memT = sbuf(st, "memT", [128, 16, 256], BF16)
              for mb in range(2):
                  load_T(gin("mem")[mb * 128:(mb + 1) * 128, :], 128, memT, mb * 128)
              S.barrier()
              gate(2)
              wx = sbuf(st, "wx", [128, 16, 1024], BF16)
              S.dma("pool", wx[:, :, 0:512], gin("w_xk").rearrange("(kc p) n -> p kc n", p=128), writes=["wx"])
              S.dma("pool", wx[:, :, 512:1024], gin("w_xv").rearrange("(kc p) n -> p kc n", p=128), writes=["wx"])
              mst = Rot([sbuf(st, "mst%d" % i, [128, 512]) for i in range(2)], "mst")
              dbgt = sbuf(st, "dbgt", [128, 4, 128])
              gate(2.2)
              S.op("dve", lambda e: e.memset(memV[:], 1.0), writes=["memV"])
              gate(2.4)
              for mb in range(2):
                  for kv in range(2):
                      p, pk = pr.next()
                      S.mm(p[:, :], [(memT[:, kc, mb * 128:(mb + 1) * 128], wx[:, kc, kv * 512:(kv + 1) * 512])
                                     for kc in range(16)], reads=["wx"], writes=[pk])
                      m, mk = mst.next()
                      S.op("act", lambda e: e.copy(out=m[:], in_=p[:]), reads=[pk], writes=[mk])
                      S.dma("sp", (mk_o, mv_o)[kv][mb * 128:(mb + 1) * 128, :], m[:], reads=[mk])
                      if kv == 1:
                          S.op("dve", lambda e: e.tensor_copy(out=memV[:, mb * 4:mb * 4 + 4, 0:128],
                                                              in_=p[:].rearrange("p (h d) -> p h d", h=4)),
                               reads=[pk], writes=["memV"])
              gate(2.5)
              gate(2.6)
              for hd in range(4):
                  p, pk = pr.next()
                  S.mm(p[:, 0:256], [(wx[:, kc, hd * 128:(hd + 1) * 128], memT[:, kc, :]) for kc in range(16)],
                       reads=["wx"], writes=[pk])
                  S.op("act", lambda e: e.copy(out=memKT[:, 0, hd, :], in_=p[:, 0:256]), reads=[pk], writes=["memKT"])
              gate(3)
              cms = sbuf(st, "cms", [128, 2, 512])
              cmb = sbuf(st, "cmb", [128, 2, 512], BF16)
              for s in range(NS):
                  S.dma("sp", cms[:], gin("cache_mem_k")[s].rearrange("(mb p) h d -> p mb (h d)", p=128), writes=["cms"])
                  S.op("dve", lambda e: e.tensor_copy(out=cmb[:], in_=cms[:]), reads=["cms"], writes=["cmb"])
                  p, pk = pr.next()
                  pb = p[:].bitcast(BF16)
                  S.transposes([(pb[:, (hd * 2 + mb) * 128:(hd * 2 + mb + 1) * 128], cmb[:, mb, hd * 128:(hd + 1) * 128])
                                for hd in range(4) for mb in range(2)], identbf, reads=["cmb", "identbf"], writes=[pk])
                  S.op("act", lambda e: e.copy(out=memKT[:, 1 + s, :, :], in_=pb[:, 0:1024].rearrange("p (h m) -> p h m", h=4)),
                       reads=[pk], writes=["memKT"])
                  S.dma("sp", cms[:], gin("cache_mem_v")[s].rearrange("(mb p) h d -> p mb (h d)", p=128), reads=[], writes=["cms"])
                  for mb in range(2):
                      S.op("dve", lambda e: e.tensor_copy(out=memV[:, ((1 + s) * 2 + mb) * 4:((1 + s) * 2 + mb) * 4 + 4, 0:128],
                                                          in_=cms[:, mb, :].rearrange("p (h d) -> p h d", h=4)),
                           reads=["cms"], writes=["memV"])
              S.barrier()
        except StopBuild:
            pass
        S.dead = False

        PH = dict(attn=True, gdn=True, post=True)
        PH.update(build.phases or {})

        if PH["attn"]:
            with contextlib.ExitStack() as st:
                WQ = Rot([sbuf(st, "wq%d" % i, [128, 16, 256], BF16) for i in range(2)], "wq")
                WKV = Rot([sbuf(st, "wkv%d" % i, [128, 16, 512], BF16) for i in range(2)], "wkv")
                qT = sbuf(st, "qT", [128, 2, T], BF16)
                kT = sbuf(st, "kT", [128, 2, T], BF16)
                Vaug = sbuf(st, "Vaug", [128, NB, 258], BF16)
                oaT = Rot([sbuf(st, "oaT%d" % i, [128, 2, T], BF16) for i in range(2)], "oaT")
                KVS = Rot([sbuf(st, "kvs%d" % i, [128, 512]) for i in range(3)], "kvs")
                ETr = Rot([sbuf(st, "ET%d" % i, [128, 2, 128], BF16) for i in range(4)], "ET")
                TMPr = Rot([sbuf(st, "tmpd%d" % i, [128, 2, 128]) for i in range(2)], "tmpd")
                subg = sbuf(st, "subg", [128, 256])
                ckb = Rot([sbuf(st, "ckb%d" % i, [128, 8, 256], BF16) for i in range(1)], "ckb")
                kTc = Rot([sbuf(st, "kTc%d" % i, [128, 2, PAST], BF16) for i in range(1)], "kTc")
                Vc = Rot([sbuf(st, "Vc%d" % i, [128, 8, 258], BF16) for i in range(1)], "Vc")
                o1r = Rot([sbuf(st, "o1_%d" % i, [128, 256]) for i in range(2)], "o1")
                o2r = Rot([sbuf(st, "o2_%d" % i, [128, 256]) for i in range(2)], "o2")
                jnk = sbuf(st, "jnk", [128, 256])
                onr = Rot([sbuf(st, "on_%d" % i, [128, 256], BF16) for i in range(2)], "on")
                smr = Rot([sbuf(st, "sm_%d" % i, [128, 8]) for i in range(3)], "sm")

                S.dma("sp", subg[:], gin("diff_subln_g")[0:1, :].partition_broadcast(128), writes=["subg"])
                S.op("dve", lambda e: e.tensor_scalar(out=subg[:], in0=subg[:], scalar1=1.0 - LAM_INIT, scalar2=None,
                                                      op0=ALU.mult), reads=["subg"], writes=["subg"])
                S.op("dve", lambda e: e.memset(Vaug[:], 1.0), writes=["Vaug"])
                for i in range(1):
                    S.op("dve", lambda e: e.memset(Vc.items[i][:], 1.0), writes=[("Vc", i)])
                ppj = psrot([0, 1, 2], "ps")
                pO = [[(ps[3], ("ps", 3)), (ps[4], ("ps", 4))], [(ps[5], ("ps", 5)), (ps[6], ("ps", 6))]]
                pT = (ps[7], ("ps", 7))
                fin = 0
                for h in range(int((build.phases or {}).get('nh', 8))):
                    wq, wqk = WQ.next()
                    wkv, wkvk = WKV.next()
                    S.dma("pool", wq[:], w_in_v()[:, :, DQ + h * 256:DQ + (h + 1) * 256], writes=[wqk])
                    S.dma("pool", wkv[:, :, 0:256], w_in_v()[:, :, DK + h * 256:DK + (h + 1) * 256], writes=[wkvk])
                    S.dma("pool", wkv[:, :, 256:512], w_in_v()[:, :, DV + h * 256:DV + (h + 1) * 256], writes=[wkvk])
                    for m in range(2):
                        for (s0, sn) in SPANS:
                            p, pk = ppj.next()
                            S.mm(p[:, :sn], [(wq[:, kc, m * 128:(m + 1) * 128], xT[:, kc, s0:s0 + sn]) for kc in range(16)],
                                 reads=[wqk], writes=[pk])
                            S.op("act", lambda e: e.activation(out=qT[:, m, s0:s0 + sn], in_=p[:, :sn], func=AF.Copy,
                                                               scale=128.0 ** -0.5), reads=[pk], writes=["qT"])
                            p, pk = ppj.next()
                            S.mm(p[:, :sn], [(wkv[:, kc, m * 128:(m + 1) * 128], xT[:, kc, s0:s0 + sn]) for kc in range(16)],
                                 reads=[wkvk], writes=[pk])
                            S.op("dve", lambda e: e.tensor_copy(out=kT[:, m, s0:s0 + sn], in_=p[:, :sn]),
                                 reads=[pk], writes=["kT"])
                    for bi, (t0, nt) in enumerate(BLOCKS):
                        p, pk = ppj.next()
                        S.mm(p[:nt, :], [(xT[:, kc, t0:t0 + nt], wkv[:, kc, :]) for kc in range(16)],
                             reads=[wkvk], writes=[pk])
                        kv, kvk = KVS.next()
                        S.op("act", lambda e: e.copy(out=kv[:nt, :], in_=p[:nt, :]), reads=[pk], writes=[kvk])
                        S.op("dve", lambda e: e.tensor_copy(out=Vaug[:nt, bi, 0:256], in_=p[:nt, 256:512]),
                             reads=[pk], writes=["Vaug"])
                        S.dma("sp", dk_o[t0:t0 + nt, h * 256:(h + 1) * 256], kv[:nt, 0:256], reads=[kvk])
                        S.dma("sp", dv_o[t0:t0 + nt, h * 256:(h + 1) * 256], kv[:nt, 256:512], reads=[kvk])
                    oa, oak = oaT.next()
                    deferred = []
                    for qi, (q0, nq) in enumerate(BLOCKS):
                        kbl = []
                        if qi < 16:
                            for kb in range(qi + 1):
                                kbl.append((kT[:, :, kb * 128:(kb + 1) * 128], Vaug[:, kb, 0:257], 128,
                                            ("v", qi - kb) if kb < qi else ("m",), ["kT", "Vaug"]))
                        else:
                            s = qi - 16
                            cb, cbk = ckb.next()
                            S.dma("pool", cb[:], gin("cache_k")[s, :, h, :].rearrange("(kb p) d -> p kb d", p=128), writes=[cbk])
                            kc_, kck = kTc.next()
                            for m in range(2):
                                tp, tpk = ppj.next()
                                tpb = tp[:].bitcast(BF16)
                                S.transposes([(tpb[:, kb * 128:(kb + 1) * 128], cb[:, kb, m * 128:(m + 1) * 128])
                                              for kb in range(8)], identbf, reads=[cbk, "identbf"], writes=[tpk])
                                S.op("act", lambda e: e.copy(out=kc_[:, m, :], in_=tpb[:, 0:1024]), reads=[tpk], writes=[kck])
                            vc, vck = Vc.next()
                            S.dma("pool", vc[:, :, 0:256], gin("cache_v")[s, :, h, :].rearrange("(kb p) d -> p kb d", p=128),
                                  writes=[vck])
                            for kb in range(8):
                                kbl.append((kc_[:, :, kb * 128:(kb + 1) * 128], vc[:, kb, 0:257], 128, ("v", 8 - kb), [kck, vck]))
                            kbl.append((kT[:, :, q0:q0 + nq], Vaug[:, qi, 0:257], nq, ("m",), ["kT", "Vaug"]))
                        (O0, O0k), (O1, O1k) = pO[qi % 2]
                        nkb = len(kbl)
                        def qk_issue(ki):
                            ksrc, vsrc, nk, bspec, kkeys = kbl[ki]
                            stp, stk = ppj.next()
                            S.mmg([(stp[:nk, m * 128:m * 128 + nq], [(ksrc[:, m, :], qT[:, m, q0:q0 + nq])]) for m in range(2)],
                                  reads=["qT"] + kkeys, writes=[stk])
                            return stp, stk
                        pend = qk_issue(0)
                        for ki, (ksrc, vsrc, nk, bspec, kkeys) in enumerate(kbl):
                            stp, stk = pend
                            if ki + 1 < nkb:
                                pend = qk_issue(ki + 1)
                            stv = stp[:nk, 0:256].rearrange("p (m q) -> p m q", m=2)[:, :, :nq]
                            et, etk = ETr.next()
                            if bspec[0] == "v":
                                col = C_AB + h * 17 + bspec[1]
                                S.op("act", lambda e: e.activation(out=et[:nk, :, :nq], in_=stv, func=AF.Exp,
                                                                   bias=cst[:nk, col:col + 1], scale=1.0),
                                     reads=[stk], writes=[etk])
                            else:
                                tm, tmk = TMPr.next()
                                bd = cst[:nk, C_BD + h * 128:C_BD + h * 128 + nq].unsqueeze(1).broadcast_to([nk, 2, nq])
                                S.op("dve", lambda e: e.tensor_tensor(out=tm[:nk, :, :nq], in0=stv, in1=bd, op=ALU.add),
                                     reads=[stk], writes=[tmk])
                                S.op("act", lambda e: e.activation(out=et[:nk, :, :nq], in_=tm[:nk, :, :nq], func=AF.Exp),
                                     reads=[tmk], writes=[etk])
                            first, last = ki == 0, ki == nkb - 1
                            for m, (O, Ok) in enumerate(((O0, O0k), (O1, O1k))):
                                S.mm1(O[:nq, 0:257], et[:nk, m, :nq], vsrc[:nk, :], first, last,
                                      reads=[etk] + kkeys, writes=([Ok] if (first or last) else []))
                        sm, smk = smr.next()
                        S.op("dve", lambda e: e.reciprocal(out=sm[:nq, 0:1], in_=O0[:nq, 256:257]), reads=[O0k], writes=[smk])
                        S.op("dve", lambda e: e.reciprocal(out=sm[:nq, 1:2], in_=O1[:nq, 256:257]), reads=[O1k], writes=[smk])
                        S.op("dve", lambda e: e.tensor_tensor(out=sm[:nq, 2:3], in0=sm[:nq, 1:2], in1=lamt[:nq, 1:2],
                                                              op=ALU.mult), reads=[smk], writes=[smk])
                        o1, o1k = o1r.next()
                        S.op("act", lambda e: e.activation(out=o1[:nq, :], in_=O0[:nq, 0:256], func=AF.Copy,
                                                           scale=sm[:nq, 0:1]), reads=[O0k, smk], writes=[o1k])
                        o2, o2k = o2r.next()
                        S.op("dve", lambda e: e.scalar_tensor_tensor(out=o2[:nq, :], in0=O1[:nq, 0:256], scalar=sm[:nq, 2:3],
                                                                     in1=o1[:nq, :], op0=ALU.mult, op1=ALU.add),
                             reads=[O1k, o1k, smk], writes=[o2k])
                        S.op("act", lambda e: e.activation(out=jnk[:nq, :], in_=o2[:nq, :], func=AF.Square,
                                                           accum_out=sm[:nq, 3:4]), reads=[o2k, smk], writes=["jnk", smk])
                        S.op("act", lambda e: e.activation(out=sm[:nq, 4:5], in_=sm[:nq, 3:4], func=AF.Sqrt,
                                                           scale=1.0 / 256, bias=1e-6), reads=[smk], writes=[smk])
                        S.op("dve", lambda e: e.reciprocal(out=sm[:nq, 5:6], in_=sm[:nq, 4:5]), reads=[smk], writes=[smk])
                        on, onk = onr.next()
                        S.op("dve", lambda e: e.scalar_tensor_tensor(out=on[:nq, :], in0=o2[:nq, :], scalar=sm[:nq, 5:6],
                                                                     in1=subg[:nq, :], op0=ALU.mult, op1=ALU.mult),
                             reads=[o2k, smk, "subg"], writes=[onk])
                        def fin_tr(on=on, onk=onk, q0=q0, nq=nq, oa=oa, oak=oak):
                            tpb = pT[0][:].bitcast(BF16)
                            S.transposes([(tpb[:, c * 128:c * 128 + nq], on[:nq, c * 128:(c + 1) * 128]) for c in range(2)],
                                         identbf, reads=[onk, "identbf"], writes=[pT[1]])
                            src = tpb[:, 0:256].rearrange("p (c q) -> p c q", c=2)[:, :, :nq]
                            S.op("act", lambda e: e.copy(out=oa[:, :, q0:q0 + nq], in_=src), reads=[pT[1]], writes=[oak])
                        deferred.append(fin_tr)
                        if len(deferred) > 1:
                            deferred.pop(0)()
                    while deferred:
                        deferred.pop(0)()
                    S.dma("sp", oaT_d[h * 256:(h + 1) * 256, :].rearrange("(c p) t -> p c t", p=128), oa[:], reads=[oak])
                S.barrier()


        if PH["gdn"]:
            with contextlib.ExitStack() as st:
                NGH = int((build.phases or {}).get("ngh", 16))
                TP = 3 + NPR + NS * (3 + TS)
                NCV = TP - 3
                SEGS = [(0, 0, NPR)] + [(NPR + 3 + (3 + TS) * s_, NPR + TS * s_, TS) for s_ in range(NS)]
                CSP = [(i, min(512, NCV - i)) for i in range(0, NCV, 512)]
                alog_b = sbuf(st, "alog_b", [128, 16])
                dtb_b = sbuf(st, "dtb_b", [128, 16])
                gng_b = sbuf(st, "gng_b", [128, 128])
                S.dma("sp", alog_b[:], gin("gdn_a_log")[0:1, :].partition_broadcast(128), writes=["alog"])
                S.dma("sp", dtb_b[:], gin("gdn_dt_bias")[0:1, :].partition_broadcast(128), writes=["dtb"])
                S.dma("sp", gng_b[:], gin("gdn_norm_g")[0:1, :].partition_broadcast(128), writes=["gng"])
                S.op("act", lambda e: e.activation(out=alog_b[:], in_=alog_b[:], func=AF.Exp), reads=["alog"], writes=["alog"])
                S.op("dve", lambda e: e.tensor_scalar(out=alog_b[:], in0=alog_b[:], scalar1=-1.0, scalar2=None, op0=ALU.mult),
                     reads=["alog"], writes=["alog"])
                wab = sbuf(st, "wab", [128, 16, 32], BF16)
                S.dma("pool", wab[:], w_in_v()[:, :, GA:GA + 32], writes=["wab"])
                gall = sbuf(st, "gall", [128, NB, 16])
                ball = sbuf(st, "ball", [128, NB, 16])
                nball = sbuf(st, "nball", [128, NB, 16])
                gcall = sbuf(st, "gcall", [128, NB, 16])
                egc = sbuf(st, "egc", [128, NB, 16])
                bgc = sbuf(st, "bgc", [128, NB, 16])
                t16 = Rot([sbuf(st, "t16_%d" % i, [128, 16]) for i in range(2)], "t16")
                for tl in (gall, ball, gcall):
                    S.op("dve", lambda e: e.memset(tl[:], 0.0), writes=["gates"])
                pg = psrot([0, 1, 2, 3], "ps")
                for bi, (t0, nt) in enumerate(BLOCKS):
                    p, pk = pg.next()
                    S.mm(p[:nt, 0:32], [(xT[:, kc, t0:t0 + nt], wab[:, kc, :]) for kc in range(16)], reads=["wab"], writes=[pk])
                    ta, tak = t16.next()
                    S.op("dve", lambda e: e.tensor_tensor(out=ta[:nt, :], in0=p[:nt, 0:16], in1=dtb_b[:nt, :], op=ALU.add),
                         reads=[pk, "dtb"], writes=[tak])
                    S.op("act", lambda e: e.activation(out=ball[:nt, bi, :], in_=p[:nt, 16:32], func=AF.Sigmoid),
                         reads=[pk, "gates"], writes=[("ball", bi)])
                    S.op("act", lambda e: e.activation(out=ta[:nt, :], in_=ta[:nt, :], func=AF.Exp), reads=[tak], writes=[tak])
                    S.op("act", lambda e: e.activation(out=ta[:nt, :], in_=ta[:nt, :], func=AF.Ln, bias=1.0), reads=[tak], writes=[tak])
                    S.op("dve", lambda e: e.tensor_tensor(out=gall[:nt, bi, :], in0=ta[:nt, :], in1=alog_b[:nt, :], op=ALU.mult),
                         reads=[tak, "alog", "gates"], writes=[("gall", bi)])
                    p2, pk2 = pg.next()
                    S.mm(p2[:nt, 0:16], [(Umask[:nt, :nt], gall[:nt, bi, :])], reads=[("gall", bi)], writes=[pk2])
                    S.op("act", lambda e: e.copy(out=gcall[:nt, bi, :], in_=p2[:nt, 0:16]), reads=[pk2, "gates"], writes=[("gc", bi)])
                S.barrier()
                S.op("act", lambda e: e.activation(out=egc[:], in_=gcall[:], func=AF.Exp), writes=["egc"])
                S.op("dve", lambda e: e.tensor_scalar(out=nball[:], in0=ball[:], scalar1=-1.0, scalar2=None, op0=ALU.mult), writes=["nball"])
                S.op("dve", lambda e: e.tensor_tensor(out=bgc[:], in0=ball[:], in1=egc[:], op=ALU.mult), reads=["egc"], writes=["bgc"])
                S.barrier()

                HG = 2
                WG = Rot([sbuf(st, "wg%d" % i, [128, 16, 128], BF16) for i in range(2)], "wg")
                preL = [sbuf(st, "pre%d" % i, [128, TP]) for i in range(2)]
                cvaL = [sbuf(st, "cva%d" % i, [128, NCV]) for i in range(1)]
                pcnt = [0]
                cvb = sbuf(st, "cvb", [128, NCV])
                cwt = Rot([sbuf(st, "cw%d" % i, [128, 4]) for i in range(2)], "cw")
                heads = []
                for j in range(HG):
                    heads.append(dict(
                        qhT=sbuf(st, "qhT%d" % j, [128, T], BF16), khT=sbuf(st, "khT%d" % j, [128, T], BF16),
                        vsT=sbuf(st, "vsT%d" % j, [128, T], BF16),
                        S=sbuf(st, "S%d" % j, [128, 128]), Sbf=sbuf(st, "Sbf%d" % j, [128, 128], BF16),
                        obT=sbuf(st, "obT%d" % j, [128, T], BF16), j=j))
                TMP = []
                for j in range(HG):
                    d = {}
                    NEU_DT = BF16 if (build.phases or {}).get("neubf", 0) else F32
                    for n in ("diag", "t1", "t2", "eT", "e", "decT", "dec", "u", "oA", "o", "jk", "PT"):
                        d[n] = sbuf(st, "g_%s%d" % (n, j), [128, 128])
                    for n in ("X", "XT", "X2", "XT2", "PTm"):
                        d[n] = sbuf(st, "g_%s%d" % (n, j), [128, 128], NEU_DT)
                    for n in ("ktm", "vb", "kbg", "kd", "qkT", "PTbf", "wT", "vnew", "ob"):
                        d[n] = sbuf(st, "g_%s%d" % (n, j), [128, 128], BF16)
                    d["sm"] = sbuf(st, "g_sm%d" % j, [128, 8])
                    TMP.append(d)
                for i_ in range(2):
                    S.op("dve", lambda e: e.memset(preL[i_][:], 0.0), writes=[("pre", i_)])

                def gdn_proj(h, hs):
                    j = hs["j"]
                    for ci, cbase in enumerate((GQ, GK, GV)):
                        col0 = cbase + h * 128
                        pcnt[0] += 1
                        pre = preL[pcnt[0] % 2]
                        cva = cvaL[0]
                        PK = ("pre", pcnt[0] % 2)
                        CK = ("cva", 0)
                        ccol = ci * 2048 + h * 128
                        wg, wgk = WG.next()
                        S.dma("pool", wg[:], w_in_v()[:, :, col0:col0 + 128], writes=[wgk])
                        cw, cwk = cwt.next()
                        S.dma("sp", cw[:], gin("conv_w")[:, ccol:ccol + 128].rearrange("j c -> c j"), writes=[cwk],
                              allow_slow_non_contiguous=True)
                        for s_ in range(NS):
                            i0 = SEGS[1 + s_][0]
                            S.dma("sp", pre[:, i0:i0 + 3], gin("state_conv")[s_, :, ccol:ccol + 128].rearrange("j c -> c j"),
                                  writes=[PK], allow_slow_non_contiguous=True)
                        for (s0, sn) in SPANS:
                            p, pk = pg.next()
                            S.mm(p[:, :sn], [(wg[:, kc, :], xT[:, kc, s0:s0 + sn]) for kc in range(16)], reads=[wgk], writes=[pk])
                            if s0 < NPR:
                                S.op("act", lambda e: e.copy(out=pre[:, 3 + s0:3 + s0 + sn], in_=p[:, :sn]), reads=[pk], writes=[PK])
                            else:
                                for s_ in range(NS):
                                    i0 = SEGS[1 + s_][0] + 3
                                    S.op("act", lambda e: e.copy(out=pre[:, i0:i0 + TS], in_=p[:, TS * s_:TS * (s_ + 1)]),
                                         reads=[pk], writes=[PK])
                        S.dma("sp", gc_o[0, :, ccol:ccol + 128].rearrange("j c -> c j"), pre[:, NPR:NPR + 3], reads=[PK],
                              allow_slow_non_contiguous=True)
                        for s_ in range(NS):
                            i0 = SEGS[1 + s_][0] + TS
                            S.dma("sp", gc_o[1 + s_, :, ccol:ccol + 128].rearrange("j c -> c j"), pre[:, i0:i0 + 3], reads=[PK],
                                  allow_slow_non_contiguous=True)
                        S.op("dve", lambda e: e.tensor_scalar(out=cva[:], in0=pre[:, 0:NCV], scalar1=cw[:, 0:1], scalar2=None,
                                                              op0=ALU.mult), reads=[PK, cwk], writes=[CK])
                        for jj in range(1, 4):
                            S.op("dve", lambda e: e.scalar_tensor_tensor(out=cva[:], in0=pre[:, jj:jj + NCV], scalar=cw[:, jj:jj + 1],
                                                                         in1=cva[:], op0=ALU.mult, op1=ALU.add),
                                 reads=[PK, cwk, CK], writes=[CK])
                        if ci == 2:
                            for (i0, tk0, n) in SEGS:
                                S.op("act", lambda e: e.activation(out=hs["vsT"][:, tk0:tk0 + n], in_=cva[:, i0:i0 + n], func=AF.Silu),
                                     reads=[CK], writes=[("vsT", j)])
                            continue
                        S.op("act", lambda e: e.activation(out=cva[:], in_=cva[:], func=AF.Silu), reads=[CK], writes=[CK])
                        S.op("dve", lambda e: e.tensor_tensor(out=cvb[:], in0=cva[:], in1=cva[:], op=ALU.mult), reads=[CK], writes=["cvb"])
                        for (i0, n) in CSP:
                            p, pk = pg.next()
                            S.mm(p[:, :n], [(ones, cvb[:, i0:i0 + n])], reads=["cvb"], writes=[pk])
                            S.op("act", lambda e: e.activation(out=pre[:, i0:i0 + n], in_=p[:, :n], func=AF.Sqrt, bias=1e-6),
                                 reads=[pk, CK], writes=[PK])
                        S.op("dve", lambda e: e.reciprocal(out=pre[:, 0:NCV], in_=pre[:, 0:NCV]), reads=[PK], writes=[PK])
                        dst = hs["qhT"] if ci == 0 else hs["khT"]
                        sc = 128.0 ** -0.5 if ci == 0 else 1.0
                        for (i0, tk0, n) in SEGS:
                            S.op("dve", lambda e: e.scalar_tensor_tensor(out=dst[:, tk0:tk0 + n], in0=cva[:, i0:i0 + n], scalar=sc,
                                                                         in1=pre[:, i0:i0 + n], op0=ALU.mult, op1=ALU.mult),
                                 reads=[CK, PK], writes=[("qk", j, ci)])
                        S.op("dve", lambda e: e.memset(pre[:, 0:3], 0.0), reads=[("qk", j, ci)], writes=[PK])

                ZG = Rot([sbuf(st, "zg2_%d" % i, [128, HG * 128]) for i in range(2)], "zg2")
                WZ2 = sbuf(st, "wz2", [128, 16, HG * 128], BF16)
                XCH = {}
                for j in range(HG):
                    for par in range(2):
                        d = {}
                        d["u"] = sbuf(st, "x_u%d%d" % (j, par), [128, 128])
                        for n in ("kd", "qkT", "wT"):
                            d[n] = sbuf(st, "x_%s%d%d" % (n, j, par), [128, 128], BF16)
                        d["sm"] = sbuf(st, "x_sm%d%d" % (j, par), [128, 8])
                        XCH[(j, par)] = d

                def gen_gz(bi, t0, nt, zg, zgk):
                    pZ, pZk = ps[3], ("ps", 3)
                    S.mm(pZ[:nt, 0:HG * 128], [(xT[:, kc, t0:t0 + nt], WZ2[:, kc, :]) for kc in range(16)], reads=["wz2"], writes=[pZk])
                    S.op("act", lambda e: e.activation(out=zg[:nt, :], in_=pZ[:nt, 0:HG * 128], func=AF.Silu), reads=[pZk], writes=[zgk])
                    yield
                    S.op("dve", lambda e: e.tensor_tensor(out=zg[:nt, :].rearrange("p (a b) -> p a b", a=HG),
                                                          in0=zg[:nt, :].rearrange("p (a b) -> p a b", a=HG),
                                                          in1=gng_b[:nt, :].unsqueeze(1).broadcast_to([nt, HG, 128]), op=ALU.mult),
                         reads=[zgk, "gng"], writes=[zgk])
                    yield

                def gen_solve(h, hs, bi, t0, nt):
                    j = hs["j"]
                    tm = TMP[j]
                    xc = XCH[(j, bi % 2)]
                    k = lambda n: ("g", n, j)
                    kx = lambda n: ("x", n, j, bi % 2)
                    psl = [(ps[4 * j + i], ("ps", 4 * j + i)) for i in range(3)]
                    cnt = [0]

                    def nps():
                        cnt[0] += 1
                        return psl[cnt[0] % 3]
                    blk = slice(t0, t0 + nt)
                    gcol = gcall[:nt, bi, h:h + 1]
                    hk = [("qk", j, 0), ("qk", j, 1), ("vsT", j)]
                    sm = xc["sm"]
                    tp, tpk = nps()
                    tpb = tp[:].bitcast(BF16)
                    S.transposes([(tpb[:nt, 0:128], hs["khT"][:, blk]), (tpb[:nt, 128:256], hs["vsT"][:, blk])], identbf,
                                 reads=hk + ["identbf"], writes=[tpk])
                    S.op("act", lambda e: e.copy(out=tm["ktm"][:nt, :], in_=tpb[:nt, 0:128]), reads=[tpk], writes=[k("ktm")])
                    S.op("dve", lambda e: e.tensor_scalar(out=tm["vb"][:nt, :], in0=tpb[:nt, 128:256], scalar1=ball[:nt, bi, h:h + 1],
                                                          scalar2=None, op0=ALU.mult), reads=[tpk], writes=[k("vb")])
                    S.op("dve", lambda e: e.tensor_scalar(out=tm["kbg"][:nt, :], in0=tpb[:nt, 0:128], scalar1=bgc[:nt, bi, h:h + 1],
                                                          scalar2=None, op0=ALU.mult), reads=[tpk], writes=[k("kbg")])
                    S.op("dve", lambda e: e.tensor_scalar(out=tm["diag"][:nt, :nt], in0=ident[:nt, :nt], scalar1=gcol, scalar2=None,
                                                          op0=ALU.mult), writes=[k("diag")])
                    yield
                    pB, pBk = nps()
                    pK, pKk = nps()
                    S.mm(pB[:, :nt], [(ones[:nt, :], tm["diag"][:nt, :nt])], reads=[k("diag")], writes=[pBk])
                    S.mmg([(pK[:nt, 0:nt], [(hs["khT"][:, blk], hs["khT"][:, blk])]),
                           (pK[:nt, 128:128 + nt], [(hs["khT"][:, blk], hs["qhT"][:, blk])])], reads=hk, writes=[pKk])
                    yield
                    S.op("dve", lambda e: e.tensor_scalar(out=tm["t1"][:nt, :nt], in0=pB[:nt, :nt], scalar1=gcol, scalar2=0.0,
                                                          op0=ALU.subtract, op1=ALU.min), reads=[pBk], writes=[k("t1")])
                    S.op("dve", lambda e: e.tensor_scalar(out=tm["t2"][:nt, :nt], in0=pB[:nt, :nt], scalar1=gcol, scalar2=0.0,
                                                          op0=ALU.subtract, op1=ALU.max), reads=[pBk], writes=[k("t2")])
                    S.op("act", lambda e: e.copy(out=sm[:, 0:1], in_=pB[:, nt - 1:nt]), reads=[pBk], writes=[kx("sm0")])
                    yield
                    S.op("act", lambda e: e.activation(out=tm["eT"][:nt, :nt], in_=tm["t1"][:nt, :nt], func=AF.Exp),
                         reads=[k("t1")], writes=[k("eT")])
                    S.op("act", lambda e: e.activation(out=tm["e"][:nt, :nt], in_=tm["t2"][:nt, :nt], func=AF.Exp, scale=-1.0),
                         reads=[k("t2")], writes=[k("e")])
                    S.op("act", lambda e: e.activation(out=sm[:nt, 1:2], in_=gcol, func=AF.Exp, scale=-1.0, bias=sm[:nt, 0:1]),
                         reads=[kx("sm0")], writes=[kx("sm1")])
                    S.op("act", lambda e: e.activation(out=sm[:, 2:3], in_=sm[:, 0:1], func=AF.Exp), reads=[kx("sm0")], writes=[kx("sm2")])
                    S.op("dve", lambda e: e.tensor_tensor(out=tm["decT"][:nt, :nt], in0=tm["eT"][:nt, :nt], in1=Umask[:nt, :nt],
                                                          op=ALU.mult), reads=[k("eT")], writes=[k("decT")])
                    S.op("dve", lambda e: e.tensor_tensor(out=tm["dec"][:nt, :nt], in0=tm["e"][:nt, :nt], in1=SLmask[:nt, :nt],
                                                          op=ALU.mult), reads=[k("e")], writes=[k("dec")])
                    yield
                    S.op("dve", lambda e: e.tensor_scalar(out=xc["kd"][:nt, :], in0=tm["ktm"][:nt, :], scalar1=sm[:nt, 1:2], scalar2=None,
                                                          op0=ALU.mult), reads=[k("ktm"), kx("sm1")], writes=[kx("kd")])
                    S.op("dve", lambda e: e.scalar_tensor_tensor(out=tm["X"][:nt, :nt], in0=pK[:nt, 0:nt], scalar=nball[:nt, bi, h:h + 1],
                                                                 in1=tm["dec"][:nt, :nt], op0=ALU.mult, op1=ALU.mult),
                         reads=[pKk, k("dec")], writes=[k("X")])
                    S.op("dve", lambda e: e.tensor_tensor(out=xc["qkT"][:nt, :nt], in0=pK[:nt, 128:128 + nt], in1=tm["decT"][:nt, :nt],
                                                          op=ALU.mult), reads=[pKk, k("decT")], writes=[kx("qkT")])
                    yield
                    pX, pXk = nps()
                    NEUBF = bool((build.phases or {}).get("neubf", 0))
                    pXv = pX[:].bitcast(BF16) if NEUBF else pX
                    S.transposes([(pXv[:nt, 0:nt], tm["X"][:nt, :nt])], identbf if NEUBF else ident, reads=[k("X"), "identbf"], writes=[pXk])
                    S.op("act", lambda e: e.copy(out=tm["XT"][:nt, :nt], in_=pXv[:nt, 0:nt]), reads=[pXk], writes=[k("XT")])
                    S.op("dve", lambda e: e.tensor_tensor(out=tm["PT"][:nt, :nt], in0=pXv[:nt, 0:nt], in1=ident[:nt, :nt], op=ALU.add),
                         reads=[pXk], writes=[k("PT")])
                    PTop = "PT"
                    if NEUBF:
                        PTop = "PTm"
                        S.op("act", lambda e: e.copy(out=tm["PTm"][:nt, :nt], in_=tm["PT"][:nt, :nt]), reads=[k("PT")], writes=[k("PTm")])
                    yield
                    NR = 6 if nt == 128 else 3
                    Xc, XTc, Xn, XTn = "X", "XT", "X2", "XT2"
                    for r in range(1, NR + 2):
                        pA, pAk = nps()
                        grp = []
                        if r <= NR:
                            grp.append((pA[:nt, 0:nt], [(tm[XTc][:nt, :nt], tm[Xc][:nt, :nt])]))
                            if r < NR:
                                grp.append((pA[:nt, 128:128 + nt], [(tm[Xc][:nt, :nt], tm[XTc][:nt, :nt])]))
                        if r >= 2:
                            grp.append((pA[:nt, 256:256 + nt], [(tm[Xc][:nt, :nt], tm[PTop][:nt, :nt])]))
                        S.mmg(grp, reads=[k(Xc), k(XTc), k(PTop)], writes=[pAk])
                        if r <= NR:
                            S.op("act", lambda e: e.copy(out=tm[Xn][:nt, :nt], in_=pA[:nt, 0:nt]), reads=[pAk], writes=[k(Xn)])
                            if r < NR:
                                S.op("act", lambda e: e.copy(out=tm[XTn][:nt, :nt], in_=pA[:nt, 128:128 + nt]), reads=[pAk], writes=[k(XTn)])
                        if r >= 2:
                            S.op("dve", lambda e: e.tensor_tensor(out=tm["PT"][:nt, :nt], in0=tm["PT"][:nt, :nt], in1=pA[:nt, 256:256 + nt],
                                                                  op=ALU.add), reads=[pAk, k("PT")], writes=[k("PT")])
                            if NEUBF and r <= NR:
                                S.op("act", lambda e: e.copy(out=tm["PTm"][:nt, :nt], in_=tm["PT"][:nt, :nt]), reads=[k("PT")], writes=[k("PTm")])
                        Xc, XTc, Xn, XTn = Xn, XTn, Xc, XTc
                        yield
                    S.op("act", lambda e: e.copy(out=tm["PTbf"][:nt, :nt], in_=tm["PT"][:nt, :nt]), reads=[k("PT")], writes=[k("PTbf")])
                    yield
                    pU, pUk = nps()
                    S.mmg([(pU[:nt, 0:128], [(tm["PTbf"][:nt, :nt], tm["vb"][:nt, :])]),
                           (pU[:, 128:128 + nt], [(tm["kbg"][:nt, :], tm["PTbf"][:nt, :nt])])],
                          reads=[k("PTbf"), k("vb"), k("kbg")], writes=[pUk])
                    S.op("act", lambda e: e.copy(out=xc["u"][:nt, :], in_=pU[:nt, 0:128]), reads=[pUk], writes=[kx("u")])
                    S.op("dve", lambda e: e.tensor_copy(out=xc["wT"][:, :nt], in_=pU[:, 128:128 + nt]), reads=[pUk], writes=[kx("wT")])
                    yield

                def gen_scan(h, hs, bi, t0, nt, zg, zgk):
                    j = hs["j"]
                    tm = TMP[j]
                    xc = XCH[(j, bi % 2)]
                    k = lambda n: ("g", n, j)
                    kx = lambda n: ("x", n, j, bi % 2)
                    pS, pSk = ps[4 * j + 3], ("ps", 4 * j + 3)
                    blk = slice(t0, t0 + nt)
                    hk = [("qk", j, 0), ("qk", j, 1), ("vsT", j)]
                    sm = xc["sm"]
                    if bi == 0:
                        S.op("dve", lambda e: e.memset(hs["S"][:], 0.0), writes=[k("S")])
                        S.op("dve", lambda e: e.memset(hs["Sbf"][:], 0.0), writes=[k("Sbf")])
                    elif bi >= 16:
                        S.dma("sp", hs["S"][:], gin("state_gdn")[bi - 16, h, :, :], writes=[k("S")])
                        S.op("act", lambda e: e.copy(out=hs["Sbf"][:], in_=hs["S"][:]), reads=[k("S")], writes=[k("Sbf")])
                    S.mmg([(pS[:nt, 0:128], [(xc["wT"][:, :nt], hs["Sbf"][:, :])]),
                           (pS[:nt, 128:256], [(hs["qhT"][:, blk], hs["Sbf"][:, :])])], reads=[kx("wT"), k("Sbf")] + hk, writes=[pSk])
                    S.op("dve", lambda e: e.tensor_tensor(out=tm["vnew"][:nt, :], in0=xc["u"][:nt, :], in1=pS[:nt, 0:128], op=ALU.subtract),
                         reads=[pSk, kx("u")], writes=[k("vnew")])
                    S.op("act", lambda e: e.activation(out=tm["oA"][:nt, :], in_=pS[:nt, 128:256], func=AF.Copy, scale=egc[:nt, bi, h:h + 1]),
                         reads=[pSk], writes=[k("oA")])
                    yield
                    S.mmg([(pS[:nt, 256:384], [(xc["qkT"][:nt, :nt], tm["vnew"][:nt, :])]),
                           (pS[:, 384:512], [(xc["kd"][:nt, :], tm["vnew"][:nt, :])])],
                          reads=[kx("qkT"), k("vnew"), kx("kd")], writes=[pSk])
                    S.op("dve", lambda e: e.scalar_tensor_tensor(out=hs["S"][:, :], in0=hs["S"][:, :], scalar=sm[:, 2:3], in1=pS[:, 384:512],
                                                                 op0=ALU.mult, op1=ALU.add), reads=[pSk, k("S"), kx("sm2")], writes=[k("S")])
                    S.op("dve", lambda e: e.tensor_tensor(out=tm["o"][:nt, :], in0=tm["oA"][:nt, :], in1=pS[:nt, 256:384], op=ALU.add),
                         reads=[pSk, k("oA")], writes=[k("o")])
                    S.op("act", lambda e: e.copy(out=hs["Sbf"][:, :], in_=hs["S"][:, :]), reads=[k("S")], writes=[k("Sbf")])
                    if bi >= 15:
                        S.dma("sp", gs_o[max(0, bi - 15), h, :, :], hs["S"][:, :], reads=[k("S")])
                    yield
                    S.op("act", lambda e: e.activation(out=tm["jk"][:nt, :], in_=tm["o"][:nt, :], func=AF.Square, accum_out=sm[:nt, 3:4]),
                         reads=[k("o")], writes=[k("jk"), kx("sm3")])
                    S.op("act", lambda e: e.activation(out=sm[:nt, 4:5], in_=sm[:nt, 3:4], func=AF.Sqrt, scale=1.0 / 128, bias=1e-6),
                         reads=[kx("sm3")], writes=[kx("sm4")])
                    yield
                    S.op("dve", lambda e: e.reciprocal(out=sm[:nt, 5:6], in_=sm[:nt, 4:5]), reads=[kx("sm4")], writes=[kx("sm5")])
                    S.op("dve", lambda e: e.scalar_tensor_tensor(out=tm["ob"][:nt, :], in0=tm["o"][:nt, :], scalar=sm[:nt, 5:6],
                                                                 in1=zg[:nt, j * 128:(j + 1) * 128], op0=ALU.mult, op1=ALU.mult),
                         reads=[k("o"), kx("sm5"), zgk], writes=[k("ob")])
                    yield
                    tqb = pS[:].bitcast(BF16)
                    S.transposes([(tqb[:, 0:nt], tm["ob"][:nt, :])], identbf, reads=[k("ob"), "identbf"], writes=[pSk])
                    S.op("act", lambda e: e.copy(out=hs["obT"][:, blk], in_=tqb[:, 0:nt]), reads=[pSk], writes=[("obT", j)])
                    yield

                def lockstep(gens):
                    live = list(gens)
                    while live:
                        nxt = []
                        for g in live:
                            try:
                                next(g)
                                nxt.append(g)
                            except StopIteration:
                                pass
                        live = nxt

                for h0 in range(0, NGH, HG):
                    hl = list(range(h0, min(NGH, h0 + HG)))
                    for h in hl:
                        gdn_proj(h, heads[h - h0])
                    S.dma("pool", WZ2[:], w_in_v()[:, :, GZ + h0 * 128:GZ + (h0 + HG) * 128], writes=["wz2"])
                    zgs = {}
                    for step in range(NB + 1):
                        gens = []
                        if step < NB:
                            t0, nt = BLOCKS[step]
                            gens += [gen_solve(h, heads[h - h0], step, t0, nt) for h in hl]
                        if step >= 1:
                            t0, nt = BLOCKS[step - 1]
                            zg, zgk = ZG.next()
                            gens += [gen_gz(step - 1, t0, nt, zg, zgk)]
                            gens += [gen_scan(h, heads[h - h0], step - 1, t0, nt, zg, zgk) for h in hl]
                        lockstep(gens)
                    for h in hl:
                        S.dma("sp", obT_d[h * 128:(h + 1) * 128, :], heads[h - h0]["obT"][:], reads=[("obT", h - h0)])
                S.barrier()


        if PH["post"]:
            with contextlib.ExitStack() as st:
                WGT = Rot([sbuf(st, "wgt%d" % i, [128, 16, 512], BF16) for i in range(2)], "wgt")
                SG = Rot([sbuf(st, "sg%d" % i, [128, T], BF16) for i in range(3)], "sg")
                pg = psrot([0, 1, 2, 3, 4, 5, 6, 7], "ps")
                for c4 in range(8):
                    wt, wtk = WGT.next()
                    S.dma("pool", wt[:], w_in_v()[:, :, GTA + c4 * 512:GTA + (c4 + 1) * 512], writes=[wtk])
                    for cc in range(4):
                        c = c4 * 4 + cc
                        sg, sgk = SG.next()
                        for (s0, sn) in SPANS:
                            p, pk = pg.next()
                            S.mm(p[:, :sn], [(wt[:, kc, cc * 128:(cc + 1) * 128], xT[:, kc, s0:s0 + sn]) for kc in range(16)],
                                 reads=[wtk], writes=[pk])
                            S.op("act", lambda e: e.activation(out=sg[:, s0:s0 + sn], in_=p[:, :sn], func=AF.Sigmoid),
                                 reads=[pk], writes=[sgk])
                        S.dma("sp", sgT_d[c * 128:(c + 1) * 128, :], sg[:], reads=[sgk], writes=["sgT_d"])
                S.barrier()
        xscope.close()

        if PH["post"]:
            lnp = gin("lnp")
            HALF = [(list(range(0, 8)), 0, 1024), (list(range(8, NB)), 1024, T - 1024)]
            for hf, (hblks, h0, hn) in enumerate(HALF):
                hspans = [(i, min(512, hn - i)) for i in range(0, hn, 512)]
                with contextlib.ExitStack() as sth:
                    bufA = sbuf(sth, "bufA", [128, 16, 1056], BF16)
                    lsm = sbuf(sth, "lsm", [128, 32])
                    L = {}

                    def alloc_ln(sc, full=True):
                        L["lng"] = sbuf(sc, "lng", [128, D])
                        L["lnb"] = sbuf(sc, "lnb", [128, D])
                        L["hp"] = Rot([sbuf(sc, "hp%d" % i, [128, D]) for i in range(2 if full else 1)], "hp")
                        if full:
                            L["xb"] = Rot([sbuf(sc, "xb%d" % i, [128, D]) for i in range(2)], "xb")
                            L["hbf"] = Rot([sbuf(sc, "hbf%d" % i, [128, D], BF16) for i in range(2)], "hbf")
                    pg = psrot([0, 1, 2, 3, 4, 5], "ps")
                    pT2 = psrot([6, 7], "ps")
                    evc = [0]

                    def load_ln(i):
                        S.dma("sp", L["lng"][:], lnp[2 * i:2 * i + 1, :].partition_broadcast(128), writes=["lng"])
                        S.dma("sp", L["lnb"][:], lnp[2 * i + 1:2 * i + 2, :].partition_broadcast(128), writes=["lnb"])

                    def ln_block(h, hk, nt, eps, dstT, lt0, out_d, t0):
                        st6 = lsm[:, 0:24].rearrange("p (c s) -> p c s", c=4)
                        for c in range(4):
                            S.op("dve", lambda e: e.bn_stats(out=st6[:nt, c, :], in_=h[:nt, c * 512:(c + 1) * 512]),
                                 reads=[hk], writes=[("lsm", c)])
                        S.op("dve", lambda e: e.bn_aggr(out=lsm[:nt, 24:26], in_=st6[:nt, :, :]),
                             reads=[("lsm", c) for c in range(4)], writes=["lmv"])
                        S.op("act", lambda e: e.activation(out=lsm[:nt, 26:27], in_=lsm[:nt, 25:26], func=AF.Sqrt, bias=eps),
                             reads=["lmv"], writes=["lrs"])
                        S.op("dve", lambda e: e.reciprocal(out=lsm[:nt, 27:28], in_=lsm[:nt, 26:27]), reads=["lrs"], writes=["lrr"])
                        S.op("dve", lambda e: e.tensor_scalar(out=h[:nt, :], in0=h[:nt, :], scalar1=lsm[:nt, 24:25], scalar2=lsm[:nt, 27:28],
                                                              op0=ALU.subtract, op1=ALU.mult), reads=[hk, "lmv", "lrr"], writes=[hk])
                        S.op("dve", lambda e: e.tensor_tensor(out=h[:nt, :], in0=h[:nt, :], in1=L["lng"][:nt, :], op=ALU.mult),
                             reads=[hk, "lng"], writes=[hk])
                        S.op("dve", lambda e: e.tensor_tensor(out=h[:nt, :], in0=h[:nt, :], in1=L["lnb"][:nt, :], op=ALU.add),
                             reads=[hk, "lnb"], writes=[hk])
                        S.dma("sp", out_d[t0:t0 + nt, :], h[:nt, :], reads=[hk], writes=[("od", t0)])
                        if dstT is None:
                            return None
                        hb, hbk = L["hbf"].next()
                        S.op("act", lambda e: e.copy(out=hb[:nt, :], in_=h[:nt, :]), reads=[hk], writes=[hbk])

                        def tail():
                            for g in range(4):
                                tp, tpk = pT2.next()
                                tpb = tp[:].bitcast(BF16)
                                S.transposes([(tpb[:, jj * 128:jj * 128 + nt], hb[:nt, (4 * g + jj) * 128:(4 * g + jj + 1) * 128])
                                              for jj in range(4)], identbf, reads=[hbk, "identbf"], writes=[tpk])
                                src = tpb[:, 0:512].rearrange("p (a b) -> p a b", a=4)[:, :, :nt]
                                dd = dstT[:, 4 * g:4 * g + 4, lt0:lt0 + nt]
                                evc[0] += 1
                                if evc[0] % 2:
                                    S.op("act", lambda e: e.copy(out=dd, in_=src), reads=[tpk], writes=[("dT", lt0)])
                                else:
                                    S.op("dve", lambda e: e.tensor_copy(out=dd, in_=src), reads=[tpk], writes=[("dT", lt0)])
                        return tail

                    with contextlib.ExitStack() as st:
                        bufB = sbuf(st, "bufB", [128, 16, 1056], BF16)
                        with contextlib.ExitStack() as s1:
                            inA = sbuf(s1, "inA", [128, 16, 1056], BF16)
                            WPA = Rot([sbuf(s1, "wpa%d" % i, [128, 16, 512], BF16) for i in range(2)], "wpa")
                            SGA = Rot([sbuf(s1, "sga%d" % i, [128, 1056], BF16) for i in range(2)], "sga")
                            m1 = Rot([sbuf(s1, "m1_%d" % i, [128, 512]) for i in range(2)], "m1")
                            for src_i, (src_d, w_d) in enumerate(((oaT_d, "w_pa"), (obT_d, "w_pb"))):
                                S.dma("sp", inA[:, :, 0:hn], src_d[:, h0:h0 + hn].rearrange("(kc p) t -> p kc t", p=128),
                                      reads=[], writes=["inA"])
                                for c4 in range(4):
                                    wt, wtk = WPA.next()
                                    S.dma("pool", wt[:], gin(w_d).rearrange("(kc p) n -> p kc n", p=128)[:, :, c4 * 512:(c4 + 1) * 512],
                                          writes=[wtk])
                                    for cc in range(4):
                                        c = c4 * 4 + cc
                                        sg, sgk = SGA.next()
                                        S.dma("sp", sg[:, 0:hn], sgT_d[(src_i * 16 + c) * 128:(src_i * 16 + c + 1) * 128, h0:h0 + hn],
                                              writes=[sgk])
                                        for (s0, sn) in hspans:
                                            p, pk = pg.next()
                                            S.mm(p[:, :sn], [(wt[:, kc, cc * 128:(cc + 1) * 128], inA[:, kc, s0:s0 + sn]) for kc in range(16)],
                                                 reads=[wtk, "inA"], writes=[pk])
                                            if src_i == 0:
                                                S.op("dve", lambda e: e.tensor_tensor(out=bufA[:, c, s0:s0 + sn], in0=p[:, :sn],
                                                                                      in1=sg[:, s0:s0 + sn], op=ALU.mult),
                                                     reads=[pk, sgk], writes=[("mix", c)])
                                            else:
                                                mm_, mk_ = m1.next()
                                                S.op("dve", lambda e: e.tensor_tensor(out=mm_[:, :sn], in0=p[:, :sn], in1=sg[:, s0:s0 + sn],
                                                                                      op=ALU.mult), reads=[pk, sgk], writes=[mk_])
                                                S.op("dve", lambda e: e.tensor_tensor(out=bufA[:, c, s0:s0 + sn], in0=bufA[:, c, s0:s0 + sn],
                                                                                      in1=mm_[:, :sn], op=ALU.add),
                                                     reads=[mk_, ("mix", c)], writes=[("mix", c)])
                            S.barrier()
                        with contextlib.ExitStack() as s2:
                            alloc_ln(s2)
                            wo = sbuf(s2, "wo", [128, 16, D], BF16)
                            S.dma("pool", wo[:], gin("w_o").rearrange("(kc p) n -> p kc n", p=128), writes=["wo"])
                            load_ln(0)
                            pend_tail = None
                            for bi in hblks:
                                t0, nt = BLOCKS[bi]
                                lt0 = t0 - h0
                                x_, xk = L["xb"].next()
                                S.dma("sp", x_[:nt, :], gin("x_all")[t0:t0 + nt, :], writes=[xk])
                                h_, hk = L["hp"].next()
                                for ct in range(4):
                                    p, pk = pg.next()
                                    S.mm(p[:nt, :], [(bufA[:, kc, lt0:lt0 + nt], wo[:, kc, ct * 512:(ct + 1) * 512]) for kc in range(16)],
                                         reads=["wo"], writes=[pk])
                                    S.op("dve", lambda e: e.scalar_tensor_tensor(out=h_[:nt, ct * 512:(ct + 1) * 512],
                                                                                 in0=x_[:nt, ct * 512:(ct + 1) * 512], scalar=ALPHA,
                                                                                 in1=p[:nt, :], op0=ALU.mult, op1=ALU.add),
                                         reads=[pk, xk], writes=[hk])
                                if pend_tail:
                                    pend_tail()
                                pend_tail = ln_block(h_, hk, nt, 1e-5, bufB, lt0, h1_d, t0)
                            if pend_tail:
                                pend_tail()
                            S.barrier()
                        with contextlib.ExitStack() as s3:
                            alloc_ln(s3)
                            wxq = sbuf(s3, "wxq", [128, 16, 512], BF16)
                            wxo = sbuf(s3, "wxo", [128, 4, D], BF16)
                            qxT = sbuf(s3, "qxT", [128, 4, 1056], BF16)
                            oxT = sbuf(s3, "oxT", [128, 4, 1056], BF16)
                            EX = Rot([sbuf(s3, "ex%d" % i, [128, 2, 128], BF16) for i in range(3)], "ex")
                            oxb = Rot([sbuf(s3, "oxb%d" % i, [128, 512], BF16) for i in range(2)], "oxb")
                            xsm = Rot([sbuf(s3, "xsm%d" % i, [128, 4]) for i in range(3)], "xsm")
                            S.dma("pool", wxq[:], gin("w_xq").rearrange("(kc p) n -> p kc n", p=128), writes=["wxq"])
                            S.dma("pool", wxo[:], gin("w_xo").rearrange("(kc p) n -> p kc n", p=128), writes=["wxo"])
                            for hd in range(4):
                                for (s0, sn) in hspans:
                                    p, pk = pg.next()
                                    S.mm(p[:, :sn], [(wxq[:, kc, hd * 128:(hd + 1) * 128], bufB[:, kc, s0:s0 + sn]) for kc in range(16)],
                                         reads=["wxq"], writes=[pk])
                                    S.op("act", lambda e: e.activation(out=qxT[:, hd, s0:s0 + sn], in_=p[:, :sn], func=AF.Copy,
                                                                       scale=128.0 ** -0.5), reads=[pk], writes=["qxT"])
                            for bi in hblks:
                                t0, nt = BLOCKS[bi]
                                lt0 = t0 - h0
                                mi = 0 if bi < 16 else 1 + (bi - 16)
                                ob_, obk = oxb.next()
                                for hd in range(4):
                                    stp, stk = pg.next()
                                    S.mmg([(stp[:, mc * 128:mc * 128 + nt], [(memKT[:, mi, hd, mc * 128:(mc + 1) * 128], qxT[:, hd, lt0:lt0 + nt])])
                                           for mc in range(2)], reads=["qxT"], writes=[stk])
                                    ex, exk = EX.next()
                                    S.op("act", lambda e: e.activation(out=ex[:, :, :nt],
                                                                       in_=stp[:, 0:256].rearrange("p (m q) -> p m q", m=2)[:, :, :nt],
                                                                       func=AF.Exp), reads=[stk], writes=[exk])
                                    po, pok = pg.next()
                                    S.mm(po[:nt, 0:129], [(ex[:, mc, :nt], memV[:, (mi * 2 + mc) * 4 + hd, 0:129]) for mc in range(2)],
                                         reads=[exk], writes=[pok])
                                    xs_, xsk = xsm.next()
                                    S.op("dve", lambda e: e.reciprocal(out=xs_[:nt, 0:1], in_=po[:nt, 128:129]), reads=[pok], writes=[xsk])
                                    S.op("act", lambda e: e.activation(out=ob_[:nt, hd * 128:(hd + 1) * 128], in_=po[:nt, 0:128], func=AF.Copy,
                                                                       scale=xs_[:nt, 0:1]), reads=[pok, xsk], writes=[obk])
                                tp, tpk = pT2.next()
                                tpb = tp[:].bitcast(BF16)
                                S.transposes([(tpb[:, hd * 128:hd * 128 + nt], ob_[:nt, hd * 128:(hd + 1) * 128]) for hd in range(4)],
                                             identbf, reads=[obk, "identbf"], writes=[tpk])
                                S.op("dve", lambda e: e.tensor_copy(out=oxT[:, :, lt0:lt0 + nt],
                                                                    in_=tpb[:, 0:512].rearrange("p (a b) -> p a b", a=4)[:, :, :nt]),
                                     reads=[tpk], writes=["oxT"])
                            load_ln(1)
                            pend_tail = None
                            for bi in hblks:
                                t0, nt = BLOCKS[bi]
                                lt0 = t0 - h0
                                x_, xk = L["xb"].next()
                                S.dma("sp", x_[:nt, :], h1_d[t0:t0 + nt, :], reads=[("od", t0)], writes=[xk])
                                h_, hk = L["hp"].next()
                                for ct in range(4):
                                    p, pk = pg.next()
                                    S.mm(p[:nt, :], [(oxT[:, kc, lt0:lt0 + nt], wxo[:, kc, ct * 512:(ct + 1) * 512]) for kc in range(4)],
                                         reads=["wxo", "oxT"], writes=[pk])
                                    S.op("dve", lambda e: e.scalar_tensor_tensor(out=h_[:nt, ct * 512:(ct + 1) * 512],
                                                                                 in0=x_[:nt, ct * 512:(ct + 1) * 512], scalar=ALPHA,
                                                                                 in1=p[:nt, :], op0=ALU.mult, op1=ALU.add),
                                         reads=[pk, xk], writes=[hk])
                                if pend_tail:
                                    pend_tail()
                                pend_tail = ln_block(h_, hk, nt, 1e-5, bufA, lt0, h2_d, t0)
                            if pend_tail:
                                pend_tail()
                            S.barrier()
                    with contextlib.ExitStack() as s5:
                        fT = sbuf(s5, "fT", [128, 44, 1056], BF16)
                        with contextlib.ExitStack() as s5a:
                            W1 = Rot([sbuf(s5a, "w1_%d" % i, [128, 16, 256], BF16) for i in range(2)], "w1")
                            W3 = Rot([sbuf(s5a, "w3_%d" % i, [128, 16, 256], BF16) for i in range(2)], "w3")
                            s1t = Rot([sbuf(s5a, "s1t%d" % i, [128, 512]) for i in range(3)], "s1t")
                            w1v = gin("w_ff1").rearrange("(kc p) n -> p kc n", p=128)
                            w3v = gin("w_ff3").rearrange("(kc p) n -> p kc n", p=128)
                            for f2 in range(22):
                                w1, w1k = W1.next()
                                w3, w3k = W3.next()
                                S.dma("pool", w1[:], w1v[:, :, f2 * 256:(f2 + 1) * 256], writes=[w1k])
                                S.dma("pool", w3[:], w3v[:, :, f2 * 256:(f2 + 1) * 256], writes=[w3k])
                                for cc in range(2):
                                    fc = f2 * 2 + cc
                                    for (s0, sn) in hspans:
                                        p1, p1k = pg.next()
                                        S.mm(p1[:, :sn], [(w1[:, kc, cc * 128:(cc + 1) * 128], bufA[:, kc, s0:s0 + sn]) for kc in range(16)],
                                             reads=[w1k], writes=[p1k])
                                        p3, p3k = pg.next()
                                        S.mm(p3[:, :sn], [(w3[:, kc, cc * 128:(cc + 1) * 128], bufA[:, kc, s0:s0 + sn]) for kc in range(16)],
                                             reads=[w3k], writes=[p3k])
                                        s1_, s1k = s1t.next()
                                        S.op("act", lambda e: e.activation(out=s1_[:, :sn], in_=p1[:, :sn], func=AF.Silu),
                                             reads=[p1k], writes=[s1k])
                                        S.op("dve", lambda e: e.tensor_tensor(out=fT[:, fc, s0:s0 + sn], in0=s1_[:, :sn], in1=p3[:, :sn],
                                                                              op=ALU.mult), reads=[s1k, p3k], writes=[("fT", fc)])
                            S.barrier()
                        with contextlib.ExitStack() as s5b:
                            W2 = Rot([sbuf(s5b, "w2_%d" % i, [128, 44, 512], BF16) for i in range(1)], "w2")
                            ypt = Rot([sbuf(s5b, "ypt%d" % i, [128, 512]) for i in range(2)], "ypt")
                            h2t = Rot([sbuf(s5b, "h2t%d" % i, [128, 512]) for i in range(2)], "h2t")
                            w2v = gin("w_ff2").rearrange("(fc p) n -> p fc n", p=128)
                            for ct in range(4):
                                w2, w2k = W2.next()
                                S.dma("pool", w2[:, 0:22, :], w2v[:, 0:22, ct * 512:(ct + 1) * 512], writes=[w2k])
                                S.dma("act", w2[:, 22:44, :], w2v[:, 22:44, ct * 512:(ct + 1) * 512], writes=[w2k]) if False else \
                                    S.dma("pool", w2[:, 22:44, :], w2v[:, 22:44, ct * 512:(ct + 1) * 512], writes=[w2k])
                                for bi in hblks:
                                    t0, nt = BLOCKS[bi]
                                    lt0 = t0 - h0
                                    p, pk = pg.next()
                                    S.mm(p[:nt, :], [(fT[:, fc, lt0:lt0 + nt], w2[:, fc, :]) for fc in range(44)], reads=[w2k], writes=[pk])
                                    hh, hhk = h2t.next()
                                    S.dma("sp", hh[:nt, :], h2_d[t0:t0 + nt, ct * 512:(ct + 1) * 512], reads=[("od", t0)], writes=[hhk])
                                    yy, yyk = ypt.next()
                                    S.op("dve", lambda e: e.scalar_tensor_tensor(out=yy[:nt, :], in0=hh[:nt, :], scalar=ALPHA, in1=p[:nt, :],
                                                                                 op0=ALU.mult, op1=ALU.add), reads=[pk, hhk], writes=[yyk])
                                    S.dma("sp", yp_d[t0:t0 + nt, ct * 512:(ct + 1) * 512], yy[:nt, :], reads=[yyk], writes=[("yp", t0)])
                            S.barrier()
                        with contextlib.ExitStack() as s5c:
                            alloc_ln(s5c, False)
                            load_ln(2)
                            for bi in hblks:
                                t0, nt = BLOCKS[bi]
                                h_, hk = L["hp"].next()
                                S.dma("sp", h_[:nt, :], yp_d[t0:t0 + nt, :], reads=[("yp", t0)], writes=[hk])
                                ln_block(h_, hk, nt, 1e-5, None, 0, y_o, t0)
                            S.barrier()

        S.finish()
    return nc


build.phases = None


def _consts():
    c = np.zeros((128, NCONST), np.float32)
    p = np.arange(128)[:, None]
    f = np.arange(128)[None, :]
    c[:, C_ID:C_ID + 128] = (p == f)
    c[:, C_ONE:C_ONE + 128] = 1.0
    c[:, C_U:C_U + 128] = (f >= p)
    c[:, C_L:C_L + 128] = (p >= f)
    c[:, C_SL:C_SL + 128] = (p > f)
    slopes = 2.0 ** (-8.0 * np.arange(1, 9) / 8.0)
    for h in range(8):
        bd = -slopes[h] * np.abs(f - p) + slopes[h] * f - 64.0 * slopes[h]
        allowed = (p // 64) <= (f // 64)
        c[:, C_BD + h * 128:C_BD + (h + 1) * 128] = np.where(allowed, bd, -30000.0)
        for d in range(17):
            c[:, C_AB + h * 17 + d] = slopes[h] * (np.arange(128) - 128.0 * d - 64.0)
    return c


_NC_CACHE = {}


def kernel(**inp):
    import ml_dtypes
    dbg = bool(build.phases and build.phases.get("dbg"))
    key = (dbg, str(build.phases))
    if key not in _NC_CACHE:
        _NC_CACHE[key] = build(dbg)
    nc = _NC_CACHE[key]
    f = lambda a: np.ascontiguousarray(a, dtype=np.float32)
    consts = _consts()
    identbf = np.eye(128, dtype=np.float32).astype(ml_dtypes.bfloat16)
    shared = {
        "w_in": f(inp["w_in"][0]), "conv_w": f(inp["conv_w"][0]),
        "lamv": f(np.concatenate([inp["lam_q1"], inp["lam_k1"], inp["lam_q2"], inp["lam_k2"]], 1)),
        "diff_subln_g": f(inp["diff_subln_g"]), "gdn_a_log": f(inp["gdn_a_log"]), "gdn_dt_bias": f(inp["gdn_dt_bias"]),
        "gdn_norm_g": f(inp["gdn_norm_g"]), "w_pa": f(inp["w_pa"][0]), "w_pb": f(inp["w_pb"][0]), "w_o": f(inp["w_o"][0]),
        "lnp": f(np.concatenate([inp["ln1_g"], inp["ln1_b"], inp["ln2_g"], inp["ln2_b"], inp["ln3_g"], inp["ln3_b"]], 0)),
        "w_xq": f(inp["w_xq"][0]), "w_xk": f(inp["w_xk"][0]), "w_xv": f(inp["w_xv"][0]), "w_xo": f(inp["w_xo"][0]),
        "w_ff1": f(inp["w_ff1"][0]), "w_ff3": f(inp["w_ff3"][0]), "w_ff2": f(inp["w_ff2"][0]),
        "consts": consts, "ident_bf": identbf,
    }
    used = set(build.used_in.keys())
    shared = {k: v for k, v in shared.items() if k in used}
    in_maps = []
    for c in range(8):
        b = c % 4
        sl = slice(2 * c, 2 * c + 2)
        m = dict(shared)
        m["x_all"] = f(np.concatenate([inp["x_prompt"][b], inp["x_sample"][sl].reshape(NS * TS, D)], 0))
        m["mem"] = f(inp["mem_prompt"][b])
        m["cache_k"] = f(inp["cache_diff_k"][0, sl])
        m["cache_v"] = f(inp["cache_diff_v"][0, sl])
        m["state_gdn"] = f(inp["state_gdn"][0, sl])
        m["state_conv"] = f(inp["state_gdn_conv"][0, sl])
        m["cache_mem_k"] = f(inp["cache_mem_k"][0, sl])
        m["cache_mem_v"] = f(inp["cache_mem_v"][0, sl])
        in_maps.append({k: v for k, v in m.items() if k in used})
    res = run_bass_kernel_spmd(nc, in_maps, core_ids=list(range(8)))
    R = res.results
    kernel.last = R
    y_p = np.stack([R[b]["y"][:NPR] for b in range(4)])
    y_s = np.concatenate([R[c]["y"][NPR:].reshape(NS, TS, D) for c in range(8)], 0)
    dk_p = np.stack([R[b]["dk"][:NPR] for b in range(4)]).reshape(1, 4, NPR, 8, 256)
    dv_p = np.stack([R[b]["dv"][:NPR] for b in range(4)]).reshape(1, 4, NPR, 8, 256)
    gs_p = np.stack([R[b]["gstate"][0] for b in range(4)])[None]
    gc_p = np.stack([R[b]["gconv"][0] for b in range(4)])[None]
    mk_p = np.stack([R[b]["memk"] for b in range(4)]).reshape(1, 4, 256, 4, 128)
    mv_p = np.stack([R[b]["memv"] for b in range(4)]).reshape(1, 4, 256, 4, 128)
    dk_s = np.concatenate([R[c]["dk"][NPR:].reshape(NS, TS, 8, 256) for c in range(8)], 0)[None]
    dv_s = np.concatenate([R[c]["dv"][NPR:].reshape(NS, TS, 8, 256) for c in range(8)], 0)[None]
    gs_s = np.concatenate([R[c]["gstate"][1:] for c in range(8)], 0)[None]
    gc_s = np.concatenate([R[c]["gconv"][1:] for c in range(8)], 0)[None]
    outs = (y_p, y_s, dk_p, dv_p, gs_p, gc_p, mk_p, mv_p, dk_s, dv_s, gs_s, gc_s)
    return tuple(np.ascontiguousarray(o, dtype=np.float32) for o in outs)
```
